# Optimizing a Trainium2 kernel written in Bass

```python
import jax, jax.numpy as jnp
from jax import lax
import numpy as np


D_MODEL = 1024
BATCH = 4
SEQ = 8192
DEPTH = 4
DEC_BATCH = 2
DEC_SEQ = 16384
PAST_LEN = 128

N_MEM = 256
XA_HEADS = 4
XA_HEAD_DIM = 128
XA_WIDTH = XA_HEADS * XA_HEAD_DIM
CHUNK = 128
SGU_WIDTH = 1536
SGU_GROUPS = 8
SGU_GROUP_DIM = SGU_WIDTH // SGU_GROUPS
MLA_HEADS = 8
Q_LORA = 256
KV_LORA = 128
QK_NOPE = 128
QK_ROPE = 64
V_HEAD = 128
ROPE_THETA = 10000.0
Q_BLOCK = 128
D_FF = 2816
N_SGU_LAYERS = (DEPTH + 1) // 2
N_MLA_LAYERS = DEPTH // 2
NORM_EPS = 1e-6

kernel_name = 'hybrid_sgu_mla_macaron_encoder'


def rms_norm(x, g):
    xf = x.astype(jnp.float32)
    y = xf * lax.rsqrt(jnp.mean(xf * xf, axis=-1, keepdims=True) + NORM_EPS)
    return (y * g.astype(jnp.float32)).astype(x.dtype)


def swiglu(h, w_in, w_out):
    gu = h @ w_in
    g, u = gu[..., :D_FF], gu[..., D_FF:]
    return (jax.nn.silu(g) * u) @ w_out


def rope_tables(seq_len):
    inv_freq = 1.0 / (ROPE_THETA ** (jnp.arange(0, QK_ROPE, 2, dtype=jnp.float32) / QK_ROPE))
    ang = jnp.arange(seq_len, dtype=jnp.float32)[:, None] * inv_freq[None, :]
    return jnp.cos(ang), jnp.sin(ang)


def apply_rope(x, cos, sin):
    half = x.shape[-1] // 2
    x1, x2 = x[..., :half], x[..., half:]
    c = cos.astype(x.dtype)
    s = sin.astype(x.dtype)
    return jnp.concatenate([x1 * c - x2 * s, x1 * s + x2 * c], axis=-1)


def memory_kv(mem, g, w):
    b, m, _ = mem.shape
    kv = rms_norm(mem, g) @ w
    k, v = kv[..., :XA_WIDTH], kv[..., XA_WIDTH:]
    return (k.reshape(b, m, XA_HEADS, XA_HEAD_DIM), v.reshape(b, m, XA_HEADS, XA_HEAD_DIM))


def memory_attention(q, mem_k, mem_v):
    b, s, _ = q.shape
    q = q.reshape(b, s, XA_HEADS, XA_HEAD_DIM) * (XA_HEAD_DIM ** -0.5)
    sc = jnp.einsum('bshd,bmhd->bhsm', q, mem_k, preferred_element_type=jnp.float32)
    p = jax.nn.softmax(sc, axis=-1).astype(mem_v.dtype)
    o = jnp.einsum('bhsm,bmhd->bshd', p, mem_v)
    return o.reshape(b, s, XA_WIDTH)


def sgu_mixer(h, mem_k, mem_v, w_in, v_norm, w_s, b_s, w_out):
    b, s, _ = h.shape
    proj = h @ w_in
    uv = jax.nn.gelu(proj[..., :2 * SGU_WIDTH])
    u, v = uv[..., :SGU_WIDTH], uv[..., SGU_WIDTH:]
    v = rms_norm(v, v_norm).reshape(b, s // CHUNK, CHUNK, SGU_GROUPS, SGU_GROUP_DIM)
    mixed = jnp.einsum('gpq,bnqgc->bnpgc', w_s, v) + b_s.T[None, None, :, :, None]
    gated = u * mixed.reshape(b, s, SGU_WIDTH)
    xa = memory_attention(proj[..., 2 * SGU_WIDTH:], mem_k, mem_v)
    return jnp.concatenate([gated, xa], axis=-1) @ w_out


def mla_mixer(h, mem_k, mem_v, cos, sin, w_in, q_norm, w_uq, kv_norm, w_uk, w_uv, w_out):
    b, s, _ = h.shape
    o1 = Q_LORA
    o2 = o1 + KV_LORA
    o3 = o2 + QK_ROPE
    proj = h @ w_in
    c_q = rms_norm(proj[..., :o1], q_norm)
    c_kv = rms_norm(proj[..., o1:o2], kv_norm)
    k_rope = apply_rope(proj[..., o2:o3], cos, sin)
    q = (c_q @ w_uq).reshape(b, s, MLA_HEADS, QK_NOPE + QK_ROPE)
    q_rope = apply_rope(q[..., QK_NOPE:], cos[:, None, :], sin[:, None, :])
    q_lat = jnp.einsum('bshn,chn->bshc', q[..., :QK_NOPE], w_uk)
    q_full = jnp.concatenate([q_lat, q_rope], axis=-1) * ((QK_NOPE + QK_ROPE) ** -0.5)
    k_full = jnp.concatenate([c_kv, k_rope], axis=-1)
    dk = KV_LORA + QK_ROPE
    q_blocks = q_full.reshape(b, s // Q_BLOCK, Q_BLOCK, MLA_HEADS, dk).transpose(1, 0, 2, 3, 4)

    def attend(qb):
        sc = jnp.einsum('bqhd,bkd->bhqk', qb, k_full, preferred_element_type=jnp.float32)
        p = jax.nn.softmax(sc, axis=-1).astype(c_kv.dtype)
        return jnp.einsum('bhqk,bkc->bqhc', p, c_kv)

    o_lat = lax.map(attend, q_blocks).transpose(1, 0, 2, 3, 4).reshape(b, s, MLA_HEADS, KV_LORA)
    o = jnp.einsum('bshc,chv->bshv', o_lat, w_uv).reshape(b, s, MLA_HEADS * V_HEAD)
    xa = memory_attention(proj[..., o3:], mem_k, mem_v)
    return jnp.concatenate([o, xa], axis=-1) @ w_out


def trunk(x, mem, p):
    cos, sin = rope_tables(x.shape[1])
    for i in range(DEPTH):
        j = i // 2
        x = x + 0.5 * swiglu(rms_norm(x, p['ffn1_norm'][i]), p['ffn1_w_in'][i], p['ffn1_w_out'][i])
        mem_k, mem_v = memory_kv(mem, p['mem_norm'][i], p['w_mem_kv'][i])
        h = rms_norm(x, p['mix_norm'][i])
        if i % 2 == 0:
            x = x + sgu_mixer(h, mem_k, mem_v, p['sgu_w_in'][j], p['sgu_v_norm'][j],
                              p['sgu_w_s'][j], p['sgu_b_s'][j], p['sgu_w_out'][j])
        else:
            x = x + mla_mixer(h, mem_k, mem_v, cos, sin, p['mla_w_in'][j], p['mla_q_norm'][j],
                              p['mla_w_uq'][j], p['mla_kv_norm'][j], p['mla_w_uk'][j],
                              p['mla_w_uv'][j], p['mla_w_out'][j])
        x = x + 0.5 * swiglu(rms_norm(x, p['ffn2_norm'][i]), p['ffn2_w_in'][i], p['ffn2_w_out'][i])
    return rms_norm(x, p['final_norm'])


def setup_inputs(seed: int = 0) -> dict:
    key = jax.random.key(seed)
    ks = jax.random.split(key, 32)

    def nrm(k, shape, scale):
        return jax.random.normal(k, shape, jnp.float32) * scale

    def gain(k, shape):
        return 1.0 + 0.02 * jax.random.normal(k, shape, jnp.float32)

    sgu_in = 2 * SGU_WIDTH + XA_WIDTH
    mla_in = Q_LORA + KV_LORA + QK_ROPE + XA_WIDTH
    return {
        'x_prompt': nrm(ks[0], (BATCH, SEQ, D_MODEL), 1.0),
        'x_sample': nrm(ks[1], (DEC_BATCH, DEC_SEQ, D_MODEL), 1.0),
        'mem_prompt': nrm(ks[2], (BATCH, N_MEM, D_MODEL), 1.0),
        'mem_sample': nrm(ks[3], (DEC_BATCH, N_MEM, D_MODEL), 1.0),
        'ffn1_norm': gain(ks[4], (DEPTH, D_MODEL)),
        'ffn1_w_in': nrm(ks[5], (DEPTH, D_MODEL, 2 * D_FF), D_MODEL ** -0.5),
        'ffn1_w_out': nrm(ks[6], (DEPTH, D_FF, D_MODEL), D_FF ** -0.5),
        'mix_norm': gain(ks[7], (DEPTH, D_MODEL)),
        'mem_norm': gain(ks[8], (DEPTH, D_MODEL)),
        'w_mem_kv': nrm(ks[9], (DEPTH, D_MODEL, 2 * XA_WIDTH), D_MODEL ** -0.5),
        'ffn2_norm': gain(ks[10], (DEPTH, D_MODEL)),
        'ffn2_w_in': nrm(ks[11], (DEPTH, D_MODEL, 2 * D_FF), D_MODEL ** -0.5),
        'ffn2_w_out': nrm(ks[12], (DEPTH, D_FF, D_MODEL), D_FF ** -0.5),
        'sgu_w_in': nrm(ks[13], (N_SGU_LAYERS, D_MODEL, sgu_in), D_MODEL ** -0.5),
        'sgu_v_norm': gain(ks[14], (N_SGU_LAYERS, SGU_WIDTH)),
        'sgu_w_s': nrm(ks[15], (N_SGU_LAYERS, SGU_GROUPS, CHUNK, CHUNK), 0.5 * CHUNK ** -0.5),
        'sgu_b_s': 1.0 + nrm(ks[16], (N_SGU_LAYERS, SGU_GROUPS, CHUNK), 0.02),
        'sgu_w_out': nrm(ks[17], (N_SGU_LAYERS, SGU_WIDTH + XA_WIDTH, D_MODEL), (SGU_WIDTH + XA_WIDTH) ** -0.5),
        'mla_w_in': nrm(ks[18], (N_MLA_LAYERS, D_MODEL, mla_in), D_MODEL ** -0.5),
        'mla_q_norm': gain(ks[19], (N_MLA_LAYERS, Q_LORA)),
        'mla_w_uq': nrm(ks[20], (N_MLA_LAYERS, Q_LORA, MLA_HEADS * (QK_NOPE + QK_ROPE)), Q_LORA ** -0.5),
        'mla_kv_norm': gain(ks[21], (N_MLA_LAYERS, KV_LORA)),
        'mla_w_uk': nrm(ks[22], (N_MLA_LAYERS, KV_LORA, MLA_HEADS, QK_NOPE), KV_LORA ** -0.5),
        'mla_w_uv': nrm(ks[23], (N_MLA_LAYERS, KV_LORA, MLA_HEADS, V_HEAD), KV_LORA ** -0.5),
        'mla_w_out': nrm(ks[24], (N_MLA_LAYERS, MLA_HEADS * V_HEAD + XA_WIDTH, D_MODEL), (MLA_HEADS * V_HEAD + XA_WIDTH) ** -0.5),
        'final_norm': gain(ks[25], (D_MODEL,)),
    }


def reference(x_prompt, x_sample, mem_prompt, mem_sample, ffn1_norm, ffn1_w_in, ffn1_w_out,
              mix_norm, mem_norm, w_mem_kv, ffn2_norm, ffn2_w_in, ffn2_w_out,
              sgu_w_in, sgu_v_norm, sgu_w_s, sgu_b_s, sgu_w_out,
              mla_w_in, mla_q_norm, mla_w_uq, mla_kv_norm, mla_w_uk, mla_w_uv, mla_w_out,
              final_norm):
    p = {
        'ffn1_norm': ffn1_norm, 'ffn1_w_in': ffn1_w_in, 'ffn1_w_out': ffn1_w_out,
        'mix_norm': mix_norm, 'mem_norm': mem_norm, 'w_mem_kv': w_mem_kv,
        'ffn2_norm': ffn2_norm, 'ffn2_w_in': ffn2_w_in, 'ffn2_w_out': ffn2_w_out,
        'sgu_w_in': sgu_w_in, 'sgu_v_norm': sgu_v_norm, 'sgu_w_s': sgu_w_s,
        'sgu_b_s': sgu_b_s, 'sgu_w_out': sgu_w_out,
        'mla_w_in': mla_w_in, 'mla_q_norm': mla_q_norm, 'mla_w_uq': mla_w_uq,
        'mla_kv_norm': mla_kv_norm, 'mla_w_uk': mla_w_uk, 'mla_w_uv': mla_w_uv,
        'mla_w_out': mla_w_out, 'final_norm': final_norm,
    }
    y_prompt = trunk(x_prompt, mem_prompt, p)
    y_sample = trunk(x_sample, mem_sample, p)
    return (y_prompt, y_sample)
```

```python
from contextlib import ExitStack
import numpy as np
import concourse.bass as bass
import concourse.mybir as mybir
from concourse.bass_utils import run_bass_kernel_spmd

F32 = mybir.dt.float32
BF16 = mybir.dt.bfloat16
AF = mybir.ActivationFunctionType
ALU = mybir.AluOpType
AX = mybir.AxisListType

D = 1024
DC = 8
DFF = 2816
FC = 22
TT = 512
NMEM = 256
EPS = 1e-6
NSLOT = 5
WBLK = 4096
G_FFN1, G_MIX, G_MEM, G_FFN2, G_FIN, G_QN, G_KVN = 0, 32, 64, 96, 128, 136, 140
NG = 142
NEG = -30000.0


class Buf:
    __slots__ = ("name", "w", "r")

    def __init__(self, name):
        self.name = name
        self.w = {}
        self.r = {}


class Sched:
    def __init__(self, nc):
        self.nc = nc
        self.eng = {"pe": nc.tensor, "act": nc.scalar, "dve": nc.vector, "pool": nc.gpsimd, "sp": nc.sync}
        self.sem = {}
        self.cnt = {}
        self.seen = {e: {} for e in self.eng}
        for e in ("pe", "act", "dve", "pool"):
            self.sem[e] = nc.alloc_semaphore("s_" + e)
            self.cnt[e] = 0

    def slot(self, name):
        key = "d_" + name
        self.sem[key] = self.nc.alloc_semaphore(key)
        self.cnt[key] = 0
        return key

    def _wait(self, eng, deps, skip=None):
        for k, v in deps.items():
            if k == skip or v <= 0:
                continue
            if eng == "pe" and k == "pe":
                continue
            if self.seen[eng].get(k, 0) >= v:
                continue
            self.eng[eng].wait_ge(self.sem[k], v)
            self.seen[eng][k] = v

    @staticmethod
    def _deps(reads, writes):
        d = {}
        for b in reads:
            for k, v in b.w.items():
                if d.get(k, 0) < v:
                    d[k] = v
        for b in writes:
            for k, v in b.w.items():
                if d.get(k, 0) < v:
                    d[k] = v
            for k, v in b.r.items():
                if d.get(k, 0) < v:
                    d[k] = v
        return d

    def _mark(self, key, v, reads, writes):
        for b in reads:
            b.r[key] = v
        for b in writes:
            b.w = {key: v}
            b.r = {}

    def op(self, eng, fn, reads=(), writes=()):
        self._wait(eng, self._deps(reads, writes))
        ins = fn(self.eng[eng])
        self.cnt[eng] += 1
        ins.then_inc(self.sem[eng], 1)
        self._mark(eng, self.cnt[eng], reads, writes)

    def mm(self, items, reads=(), writes=()):
        self._wait("pe", self._deps(reads, writes))
        ins = None
        for (o, l, r, st, sp_) in items:
            ins = self.nc.tensor.matmul(o, lhsT=l, rhs=r, start=st, stop=sp_)
        self.cnt["pe"] += 1
        ins.then_inc(self.sem["pe"], 1)
        self._mark("pe", self.cnt["pe"], reads, writes)

    def tr(self, items, reads=(), writes=()):
        self._wait("pe", self._deps(reads, writes))
        ins = None
        for (o, i, idn) in items:
            ins = self.nc.tensor.transpose(o, i, idn)
        self.cnt["pe"] += 1
        ins.then_inc(self.sem["pe"], 1)
        self._mark("pe", self.cnt["pe"], reads, writes)

    def dma(self, q, out, in_, slot, reads=(), writes=()):
        self._wait(q, self._deps(reads, writes), skip=slot)
        self.eng[q].dma_start(out=out, in_=in_).then_inc(self.sem[slot], 16)
        self.cnt[slot] += 16
        self._mark(slot, self.cnt[slot], reads, writes)

    def barrier(self):
        for e in self.eng:
            self._wait(e, dict(self.cnt))


class Alloc:
    def __init__(self, nc, prefix):
        self.nc = nc
        self.prefix = prefix
        self.es = ExitStack()

    def sb(self, name, shape, dt):
        return self.es.enter_context(self.nc.sbuf_tensor(self.prefix + name, list(shape), dt))

    def close(self):
        self.es.close()


def _stream_names(tag):
    def ffn(l, w):
        return [("ffn", l, w, "in", k) for k in range(11)] + [("ffn", l, w, "out", dc) for dc in range(8)]

    def sgu(j):
        return ([("sgu", j, "v", vb) for vb in range(3)] + [("sgu", j, "u", g2) for g2 in range(4)]
                + [("sgu", j, "xaq", 0), ("sgu", j, "ws", 0)] + [("sgu", j, "out", dc) for dc in range(8)])

    def mlap(j):
        return [("mla", j, "in", 0), ("mla", j, "in", 1), ("mla", j, "uq", 0), ("mla", j, "uk", 0)]

    def mlao(j):
        return [("mla", j, "uv", 0)] + [("mla", j, "out", k) for k in range(4)]

    if tag == "P0":
        return [("mem", l, kv, 0) for l in range(4) for kv in ("k", "v")]
    if tag == "A":
        return ffn(0, 1) + sgu(0) + ffn(0, 2) + ffn(1, 1) + mlap(0)
    if tag == "B2":
        return mlao(0) + ffn(1, 2) + ffn(2, 1) + sgu(1) + ffn(2, 2) + ffn(3, 1) + mlap(1)
    if tag == "C2":
        return mlao(1) + ffn(3, 2)
    raise ValueError(tag)


class Builder:
    def __init__(self, TOK):
        self.TOK = TOK
        self.NT = TOK // TT
        self.NK = TOK
        self.NKT = self.NK // 128
        nc = bass.Bass("TRN2", target_bir_lowering=False)
        self.nc = nc
        self.S = Sched(nc)
        self.declare()
        self.plan_blocks()

    def declare(self):
        nc, TOK, NT = self.nc, self.TOK, self.NT

        def inp(name, shape, dt=F32):
            return nc.dram_tensor(name, list(shape), dt, kind="ExternalInput")

        self.x = inp("x", [TOK, D])
        self.mem = inp("mem", [NMEM, D])
        self.ropeC = inp("ropeC", [64, TOK])
        self.ropeS = inp("ropeS", [64, TOK])
        self.kmask = inp("kmask", [128, self.NKT])
        self.ident = inp("ident", [128, 128])
        self.gpack = inp("gpack", [128, NG])
        self.vnb = inp("vnb", [2, 128, 1536])
        self.bsb = inp("bsb", [2, 128, 1024])
        self.y = nc.dram_tensor("y", [TOK, D], F32, kind="ExternalOutput")

        def scr(name, shape, dt):
            return nc.dram_tensor(name, list(shape), dt, kind="Internal")

        self.xscr = scr("xscr", [NT, 128, DC * TT], F32)
        self.qscr = scr("qscr", [NT, 128, 16 * TT], BF16)
        self.xascr = scr("xascr", [NT, 128, 4 * TT], BF16)
        self.olscr = scr("olscr", [NT, 128, 8 * TT], BF16)
        self.exin = scr("exin", [320, TOK], BF16)
        self.xscr_b = [Buf("xscr%d" % t) for t in range(NT)]
        self.qscr_b = [Buf("qscr%d" % t) for t in range(NT)]
        self.xascr_b = [Buf("xascr%d" % t) for t in range(NT)]
        self.olscr_b = [Buf("olscr%d" % t) for t in range(NT)]
        self.ex_b = [[Buf("ex%d_%d" % (t, i)) for i in range(3)] for t in range(NT)]

    def plan_blocks(self):
        self.blocks = {nm: (i, n) for i, (nm, n) in enumerate(_block_table())}
        self.wsrc = self.nc.dram_tensor("wsrc", [NBLK, 128, WBLK], F32, kind="ExternalInput")
        self.wscr = self.nc.dram_tensor("wscr", [NBLK, 128, WBLK], BF16, kind="Internal")
        self.wscr_b = Buf("wscr")

    def wblk(self, bid, n):
        return self.wscr[bid, :, 0:n]

    def setup(self):
        nc, S = self.nc, self.S
        G = Alloc(nc, "g_")
        self.G = G
        self.gp = G.sb("gp", [128, NG], F32)
        self.identf = G.sb("identf", [128, 128], F32)
        self.identb = G.sb("identb", [128, 128], BF16)
        self.ones = {n: G.sb("ones%d" % n, [128, 128], BF16) for n in (1, 128, 256, 1024)}
        self.onesf = G.sb("onesf", [128, 128], F32)
        self.KmT = G.sb("KmT", [128, 4, 4, NMEM], BF16)
        self.Vm = G.sb("Vm", [128, 4, 2, 512], BF16)
        self.kmb = [Buf("kmT%d" % l) for l in range(4)]
        self.vmb = [Buf("vm%d" % l) for l in range(4)]
        self.ps = nc.alloc_psum_tensor("ps", [128, 7, 512], F32)
        self.psb = nc.alloc_psum_tensor("psb", [128, 1024], BF16)
        self.pb = [Buf("ps%d" % i) for i in range(7)]
        self.psbb = Buf("psb")
        self.rr = {"mm": [0, 1, 2, 3], "acc": [4, 5]}
        self.rri = {"mm": 0, "acc": 0}
        self.cb = Buf("consts")
        sl = S.slot("const")
        S.dma("sp", self.gp[:, :], self.gpack[:, :], sl, writes=[self.cb])
        S.dma("sp", self.identf[:, :], self.ident[:, :], sl, writes=[self.cb])
        S.op("dve", lambda e: e.tensor_copy(out=self.identb[:, :], in_=self.identf[:, :]), reads=[self.cb], writes=[self.cb])
        for n, t in self.ones.items():
            S.op("pool", lambda e, t=t, n=n: e.memset(t[:, :], 1.0 / n), writes=[self.cb])
        S.op("pool", lambda e: e.memset(self.onesf[:, :], 1.0), writes=[self.cb])
        S.barrier()

    def bank(self, grp):
        b = self.rr[grp][self.rri[grp] % len(self.rr[grp])]
        self.rri[grp] += 1
        return b

    def start_stream(self, A, tag, reps):
        self.wslot_t = [A.sb("wslot%d" % i, [128, WBLK], BF16) for i in range(NSLOT)]
        self.wslot_b = [Buf("wslot%d" % i) for i in range(NSLOT)]
        if not hasattr(self, "wslot_s"):
            self.wslot_s = [self.S.slot("wslot%d" % i) for i in range(NSLOT)]
        self.seq = _stream_names(tag) * reps
        self.seq_i = 0
        self.seq_loaded = 0

    def wnext(self, name):
        assert self.seq[self.seq_i] == name, (self.seq[self.seq_i], name)
        i = self.seq_i
        while self.seq_loaded < min(len(self.seq), i + NSLOT - 1):
            k = self.seq_loaded
            bid, n = self.blocks[self.seq[k]]
            s = k % NSLOT
            self.S.dma("sp", self.wslot_t[s][:, 0:n], self.wblk(bid, n), self.wslot_s[s],
                       reads=[self.wscr_b], writes=[self.wslot_b[s]])
            self.seq_loaded += 1
        self.seq_i += 1
        return self.wslot_b[i % NSLOT], self.wslot_t[i % NSLOT]

    def convert(self):
        S = self.S
        A = Alloc(self.nc, "cv_")
        stf = [A.sb("stf%d" % i, [128, WBLK], F32) for i in range(2)]
        stb = [A.sb("stb%d" % i, [128, WBLK], BF16) for i in range(2)]
        stfb = [Buf("stf%d" % i) for i in range(2)]
        stbb = [Buf("stb%d" % i) for i in range(2)]
        sl_in = [S.slot("cvi%d" % i) for i in range(2)]
        sl_out = [S.slot("cvo%d" % i) for i in range(2)]
        engs = ["dve", "act", "pool"]
        for bid in range(NBLK):
            s = bid % 2
            S.dma("sp", stf[s][:, :], self.wsrc[bid], sl_in[s], writes=[stfb[s]])
            e = engs[bid % 3]
            if e == "act":
                S.op("act", lambda en, s=s: en.activation(out=stb[s][:, :], in_=stf[s][:, :], func=AF.Copy),
                     reads=[stfb[s]], writes=[stbb[s]])
            else:
                S.op(e, lambda en, s=s: en.tensor_copy(out=stb[s][:, :], in_=stf[s][:, :]), reads=[stfb[s]], writes=[stbb[s]])
            S.dma("pool", self.wscr[bid], stb[s][:, :], sl_out[s], reads=[stbb[s]])
        self.wscr_b.w = {sl_out[0]: S.cnt[sl_out[0]], sl_out[1]: S.cnt[sl_out[1]]}
        S.barrier()
        A.close()

    def rmsnorm(self, T, srcs, gcol, nfeat, outs, N=TT):
        S, ps = self.S, self.ps
        n = len(srcs)
        for c, (b, ap, rows) in enumerate(srcs):
            S.op("act", lambda e, c=c, ap=ap, rows=rows: e.activation(out=T["sq"][0:rows, c, 0:N], in_=ap, func=AF.Square),
                 reads=[b], writes=[T["sqb"][c]])
        S.mm([(ps[:, 6, 0:N], self.ones[nfeat][0:srcs[c][2], :], T["sq"][0:srcs[c][2], c, 0:N], c == 0, c == n - 1) for c in range(n)],
             reads=T["sqb"][:n], writes=[self.pb[6]])
        S.op("act", lambda e: e.activation(out=T["r"][:, 0:N], in_=ps[:, 6, 0:N], func=AF.Sqrt, bias=EPS, scale=1.0),
             reads=[self.pb[6]], writes=[T["rb"]])
        S.op("dve", lambda e: e.reciprocal(out=T["r"][:, 0:N], in_=T["r"][:, 0:N]), reads=[T["rb"]], writes=[T["rb"]])
        for c, (b, ap, rows) in enumerate(srcs):
            ob, oap = outs[c]
            S.op("dve", lambda e, c=c, ap=ap, rows=rows, oap=oap: e.scalar_tensor_tensor(
                out=oap, in0=ap, scalar=self.gp[0:rows, gcol + c:gcol + c + 1], in1=T["r"][0:rows, 0:N],
                op0=ALU.mult, op1=ALU.mult), reads=[b, T["rb"]], writes=[ob])

    def xnorm(self, T, gcol):
        self.rmsnorm(T, [(T["xb"][c], T["xT"][:, c, :], 128) for c in range(DC)], gcol, 1024,
                     [(T["xhb"][c], T["xh"][:, c, :]) for c in range(DC)])

    def tmpf(self, T):
        i = T["tmpi"] % 2
        T["tmpi"] += 1
        return T["tmp"][i], T["tmpb"][i]

    def pbuf(self, T):
        i = T["pti"] % 3
        T["pti"] += 1
        return T["pt"][i], T["ptb"][i]

    def ffn(self, T, l, which):
        S, ps, pb = self.S, self.ps, self.pb
        big, bigb, xh, xhb = T["big"], T["bigb"], T["xh"], T["xhb"]
        self.xnorm(T, (G_FFN1 if which == 1 else G_FFN2) + l * 8)
        for k in range(11):
            wb, w = self.wnext(("ffn", l, which, "in", k))
            for jj in range(2):
                j = 2 * k + jj
                bg, bu = self.bank("mm"), self.bank("mm")
                for half, bk in ((0, bg), (1, bu)):
                    S.mm([(ps[:, bk, :], w[:, (jj * 8 + dc) * 256 + half * 128:(jj * 8 + dc) * 256 + half * 128 + 128],
                           xh[:, dc, :], dc == 0, dc == 7) for dc in range(DC)], reads=[wb] + xhb, writes=[pb[bk]])
                tp, tb = self.tmpf(T)
                S.op("act", lambda e, bg=bg, tp=tp: e.activation(out=tp[:, :], in_=ps[:, bg, :], func=AF.Silu),
                     reads=[pb[bg]], writes=[tb])
                S.op("dve", lambda e, bu=bu, tp=tp, j=j: e.tensor_tensor(out=big[:, j, :], in0=tp[:, :], in1=ps[:, bu, :], op=ALU.mult),
                     reads=[tb, pb[bu]], writes=[bigb[j]])
        for dc in range(DC):
            wb, w = self.wnext(("ffn", l, which, "out", dc))
            bo = self.bank("acc")
            S.mm([(ps[:, bo, :], w[:, j * 128:(j + 1) * 128], big[:, j, :], j == 0, j == FC - 1) for j in range(FC)],
                 reads=[wb] + bigb[:FC], writes=[pb[bo]])
            S.op("dve", lambda e, bo=bo, dc=dc: e.scalar_tensor_tensor(
                out=T["xT"][:, dc, :], in0=ps[:, bo, :], scalar=0.5, in1=T["xT"][:, dc, :], op0=ALU.mult, op1=ALU.add),
                reads=[pb[bo], T["xb"][dc]], writes=[T["xb"][dc]])

    def xa_attn(self, T, l):
        S, ps, pb = self.S, self.ps, self.pb
        big, bigb = T["big"], T["bigb"]
        for h in range(4):
            qi = 16 + h
            pts = []
            for mc in range(2):
                sbk = self.bank("mm")
                S.mm([(ps[:, sbk, :], self.KmT[:, l, h, mc * 128:(mc + 1) * 128], big[:, qi, :], True, True)],
                     reads=[bigb[qi], self.kmb[l]], writes=[pb[sbk]])
                pt, ptb = self.pbuf(T)
                S.op("act", lambda e, sbk=sbk, pt=pt: e.activation(out=pt[:, :], in_=ps[:, sbk, :], func=AF.Exp, scale=128.0 ** -0.5),
                     reads=[pb[sbk]], writes=[ptb])
                pts.append((pt, ptb))
            bo = self.bank("acc")
            S.mm([(ps[:, bo, :], self.Vm[:, l, mc, h * 128:(h + 1) * 128], pts[mc][0][:, :], mc == 0, mc == 1) for mc in range(2)]
                 + [(ps[:, 6, :], self.ones[1][:, :], pts[mc][0][:, :], mc == 0, mc == 1) for mc in range(2)],
                 reads=[pts[0][1], pts[1][1], self.vmb[l]], writes=[pb[bo], pb[6]])
            S.op("dve", lambda e: e.reciprocal(out=T["rtmp"][:, :], in_=ps[:, 6, :]), reads=[pb[6]], writes=[T["rtmpb"]])
            S.op("dve", lambda e, bo=bo, qi=qi: e.tensor_tensor(out=big[:, qi, :], in0=ps[:, bo, :], in1=T["rtmp"][:, :], op=ALU.mult),
                 reads=[pb[bo], T["rtmpb"]], writes=[bigb[qi]])

    def sgu(self, T, j):
        S, ps, pb = self.S, self.ps, self.pb
        big, bigb, xh, xhb = T["big"], T["bigb"], T["xh"], T["xhb"]
        l = 2 * j
        self.xnorm(T, G_MIX + l * 8)
        vtf, vtfb, vh, vhb = T["vtf"], T["vtfb"], T["vh"], T["vhb"]
        for vb in range(3):
            wb, w = self.wnext(("sgu", j, "v", vb))
            for tc in range(4):
                bk = self.bank("mm")
                S.mm([(ps[:, bk, :], xh[:, dc, tc * 128:(tc + 1) * 128], w[:, dc * 512:(dc + 1) * 512], dc == 0, dc == 7)
                      for dc in range(DC)], reads=[wb] + xhb, writes=[pb[bk]])
                S.op("act", lambda e, bk=bk, tc=tc, vb=vb: e.activation(out=vtf[:, tc, vb * 512:(vb + 1) * 512], in_=ps[:, bk, :],
                                                                       func=AF.Gelu_apprx_tanh), reads=[pb[bk]], writes=[vtfb[tc][vb]])
        for tc in range(4):
            S.op("act", lambda e, tc=tc: e.activation(out=T["vsq"][:, :], in_=vtf[:, tc, :], func=AF.Square),
                 reads=vtfb[tc], writes=[T["vsqb"]])
            S.op("dve", lambda e, tc=tc: e.reduce_sum(out=T["ssv"][:, tc:tc + 1], in_=T["vsq"][:, :], axis=AX.X),
                 reads=[T["vsqb"]], writes=[T["ssvb"]])
            S.op("act", lambda e, tc=tc: e.activation(out=T["ssv"][:, tc:tc + 1], in_=T["ssv"][:, tc:tc + 1], func=AF.Sqrt,
                                                      bias=EPS, scale=1.0 / 1536), reads=[T["ssvb"]], writes=[T["ssvb"]])
            S.op("dve", lambda e, tc=tc: e.reciprocal(out=T["ssv"][:, tc:tc + 1], in_=T["ssv"][:, tc:tc + 1]),
                 reads=[T["ssvb"]], writes=[T["ssvb"]])
            S.op("dve", lambda e, tc=tc: e.scalar_tensor_tensor(out=vh[:, tc, :], in0=vtf[:, tc, :], scalar=T["ssv"][:, tc:tc + 1],
                                                                in1=T["gvb"][:, :], op0=ALU.mult, op1=ALU.mult),
                 reads=vtfb[tc] + [T["ssvb"], T["sgc"]], writes=[vhb[tc]])
        for g2 in range(4):
            wb, w = self.wnext(("sgu", j, "u", g2))
            for gg in range(2):
                g = 2 * g2 + gg
                for (ti, rows, off) in ((g, 128, 0), (8 + g, 64, 128)):
                    bk = self.bank("mm")
                    S.mm([(ps[0:rows, bk, :], w[:, (gg * 8 + dc) * 192 + off:(gg * 8 + dc) * 192 + off + rows], xh[:, dc, :],
                           dc == 0, dc == 7) for dc in range(DC)], reads=[wb] + xhb, writes=[pb[bk]])
                    S.op("act", lambda e, bk=bk, ti=ti, rows=rows: e.activation(out=big[0:rows, ti, :], in_=ps[0:rows, bk, :],
                                                                                func=AF.Gelu_apprx_tanh), reads=[pb[bk]], writes=[bigb[ti]])
        wb, w = self.wnext(("sgu", j, "xaq", 0))
        for h in range(4):
            bk = self.bank("mm")
            S.mm([(ps[:, bk, :], w[:, (h * 8 + dc) * 128:(h * 8 + dc + 1) * 128], xh[:, dc, :], dc == 0, dc == 7) for dc in range(DC)],
                 reads=[wb] + xhb, writes=[pb[bk]])
            S.op("dve", lambda e, bk=bk, h=h: e.tensor_copy(out=big[:, 16 + h, :], in_=ps[:, bk, :]), reads=[pb[bk]], writes=[bigb[16 + h]])
        wb, w = self.wnext(("sgu", j, "ws", 0))
        for g in range(8):
            for (ti, rows, off) in ((g, 128, 0), (8 + g, 64, 128)):
                bk = self.bank("mm")
                S.mm([(ps[0:rows, bk, tc * 128:(tc + 1) * 128], vh[:, tc, 192 * g + off:192 * g + off + rows],
                       w[:, g * 128:(g + 1) * 128], True, True) for tc in range(4)], reads=[wb] + vhb, writes=[pb[bk]])
                tp, tb = self.tmpf(T)
                bias = T["bsbt"][0:rows, g * 128:(g + 1) * 128].unsqueeze(1).broadcast_to([rows, 4, 128])
                S.op("dve", lambda e, bk=bk, tp=tp, rows=rows, bias=bias: e.tensor_tensor(
                    out=tp[0:rows, :].rearrange("p (a b) -> p a b", a=4), in0=ps[0:rows, bk, :].rearrange("p (a b) -> p a b", a=4),
                    in1=bias, op=ALU.add), reads=[pb[bk], T["sgc"]], writes=[tb])
                S.op("dve", lambda e, tp=tp, ti=ti, rows=rows: e.tensor_tensor(out=big[0:rows, ti, :], in0=tp[0:rows, :],
                                                                               in1=big[0:rows, ti, :], op=ALU.mult),
                     reads=[tb, bigb[ti]], writes=[bigb[ti]])
        self.xa_attn(T, l)
        for dc in range(DC):
            wb, w = self.wnext(("sgu", j, "out", dc))
            bo = self.bank("acc")
            items = []
            for t in range(20):
                rows = 64 if 8 <= t < 16 else 128
                items.append((ps[:, bo, :], w[0:rows, t * 128:(t + 1) * 128], big[0:rows, t, :], t == 0, t == 19))
            S.mm(items, reads=[wb] + bigb[:20], writes=[pb[bo]])
            S.op("dve", lambda e, bo=bo, dc=dc: e.tensor_tensor(out=T["xT"][:, dc, :], in0=ps[:, bo, :], in1=T["xT"][:, dc, :], op=ALU.add),
                 reads=[pb[bo], T["xb"][dc]], writes=[T["xb"][dc]])

    def rope(self, T, ba, bb_, out_ap, out_b):
        S, ps, pb = self.S, self.ps, self.pb
        S.op("dve", lambda e: e.tensor_tensor(out=T["rt1"][0:64, :], in0=ps[0:64, ba, :], in1=T["rC"][0:64, :], op=ALU.mult),
             reads=[pb[ba], T["ropeb"]], writes=[T["rt1b"]])
        S.op("dve", lambda e: e.tensor_tensor(out=T["rt2"][0:64, :], in0=ps[0:64, bb_, :], in1=T["rS"][0:64, :], op=ALU.mult),
             reads=[pb[bb_], T["ropeb"]], writes=[T["rt2b"]])
        S.op("pool", lambda e: e.tensor_tensor(out=out_ap, in0=T["rt1"][0:64, :], in1=T["rt2"][0:64, :], op=ALU.add),
             reads=[T["rt1b"], T["rt2b"]], writes=[out_b])

    def mla_proj(self, T, j, t):
        S, ps, pb = self.S, self.ps, self.pb
        big, bigb, xh, xhb = T["big"], T["bigb"], T["xh"], T["xhb"]
        l = 2 * j + 1
        cols = slice(t * TT, (t + 1) * TT)
        S.dma("pool", T["rC"][0:64, :], self.ropeC[:, cols], T["sl_rope"], writes=[T["ropeb"]])
        S.dma("pool", T["rS"][0:64, :], self.ropeS[:, cols], T["sl_rope"], writes=[T["ropeb"]])
        self.xnorm(T, G_MIX + l * 8)
        wb, w = self.wnext(("mla", j, "in", 0))
        cqf, cqfb = T["cqf"], T["cqfb"]
        for u in range(3):
            bk = self.bank("mm")
            S.mm([(ps[:, bk, :], w[:, (u * 8 + dc) * 128:(u * 8 + dc + 1) * 128], xh[:, dc, :], dc == 0, dc == 7) for dc in range(DC)],
                 reads=[wb] + xhb, writes=[pb[bk]])
            S.op("act", lambda e, bk=bk, u=u: e.activation(out=cqf[:, u, :], in_=ps[:, bk, :], func=AF.Copy), reads=[pb[bk]], writes=[cqfb[u]])
        ba, bb_ = self.bank("mm"), self.bank("mm")
        for off, bk in ((0, ba), (64, bb_)):
            S.mm([(ps[0:64, bk, :], w[:, (24 + dc) * 128 + off:(24 + dc) * 128 + off + 64], xh[:, dc, :], dc == 0, dc == 7)
                  for dc in range(DC)], reads=[wb] + xhb, writes=[pb[bk]])
        self.rope(T, ba, bb_, T["kr"][0:64, :], T["krb"])
        wb, w = self.wnext(("mla", j, "in", 1))
        for h in range(4):
            bk = self.bank("mm")
            S.mm([(ps[:, bk, :], w[:, (h * 8 + dc) * 128:(h * 8 + dc + 1) * 128], xh[:, dc, :], dc == 0, dc == 7) for dc in range(DC)],
                 reads=[wb] + xhb, writes=[pb[bk]])
            S.op("dve", lambda e, bk=bk, h=h: e.tensor_copy(out=big[:, 16 + h, :], in_=ps[:, bk, :]), reads=[pb[bk]], writes=[bigb[16 + h]])
        self.rmsnorm(T, [(cqfb[u], cqf[:, u, :], 128) for u in range(2)], G_QN + 2 * j, 256,
                     [(T["cqhb"][u], T["cqh"][:, u, :]) for u in range(2)])
        self.rmsnorm(T, [(cqfb[2], cqf[:, 2, :], 128)], G_KVN + j, 128, [(T["ckvb"], T["ckv"][:, :])])
        wb, w = self.wnext(("mla", j, "uq", 0))
        wkb, wk = self.wnext(("mla", j, "uk", 0))
        for h in range(8):
            bk = self.bank("mm")
            S.mm([(ps[:, bk, :], w[:, (h * 2 + kc) * 256:(h * 2 + kc) * 256 + 128], T["cqh"][:, kc, :], kc == 0, kc == 1) for kc in range(2)],
                 reads=[wb] + T["cqhb"], writes=[pb[bk]])
            qn, qnb = T["qn"][h % 2], T["qnb"][h % 2]
            S.op("act", lambda e, bk=bk, qn=qn: e.activation(out=qn[:, :], in_=ps[:, bk, :], func=AF.Copy), reads=[pb[bk]], writes=[qnb])
            ba, bb_ = self.bank("mm"), self.bank("mm")
            for off, bk2 in ((128, ba), (192, bb_)):
                S.mm([(ps[0:64, bk2, :], w[:, (h * 2 + kc) * 256 + off:(h * 2 + kc) * 256 + off + 64], T["cqh"][:, kc, :], kc == 0, kc == 1)
                      for kc in range(2)], reads=[wb] + T["cqhb"], writes=[pb[bk2]])
            self.rope(T, ba, bb_, big[0:64, 8 + h, :], bigb[8 + h])
            bk = self.bank("mm")
            S.mm([(ps[:, bk, :], wk[:, h * 128:(h + 1) * 128], qn[:, :], True, True)], reads=[wkb, qnb], writes=[pb[bk]])
            S.op("act", lambda e, bk=bk, h=h: e.activation(out=big[:, h, :], in_=ps[:, bk, :], func=AF.Copy), reads=[pb[bk]], writes=[bigb[h]])
        S.dma("pool", self.qscr[t].rearrange("p (a b) -> p a b", a=16), big[:, 0:16, :], T["sl_q"], reads=bigb[0:16], writes=[self.qscr_b[t]])
        S.dma("pool", self.exin[0:128, cols], T["ckv"][:, :], T["sl_k0"], reads=[T["ckvb"]], writes=[self.ex_b[t][0]])
        S.dma("pool", self.exin[128:192, cols], T["kr"][0:64, :], T["sl_k1"], reads=[T["krb"]], writes=[self.ex_b[t][1]])
        S.tr([(self.psb[:, tc * 128:(tc + 1) * 128], T["ckv"][:, tc * 128:(tc + 1) * 128], self.identb[:, :]) for tc in range(4)],
             reads=[T["ckvb"]], writes=[self.psbb])
        S.op("act", lambda e: e.activation(out=T["vst"][:, :], in_=self.psb[:, 0:512], func=AF.Copy), reads=[self.psbb], writes=[T["vstb"]])
        S.dma("pool", self.exin[192:320, cols], T["vst"][:, :], T["sl_v"], reads=[T["vstb"]], writes=[self.ex_b[t][2]])
        self.xa_attn(T, l)
        S.dma("pool", self.xascr[t].rearrange("p (a b) -> p a b", a=4), big[:, 16:20, :], T["sl_xa"], reads=bigb[16:20], writes=[self.xascr_b[t]])
        S.dma("pool", self.xscr[t].rearrange("p (a b) -> p a b", a=DC), T["xT"][:, :, :], T["sl_xst"], reads=T["xb"], writes=[self.xscr_b[t]])

    def mla_out(self, T, j, t):
        S, ps, pb = self.S, self.ps, self.pb
        big, bigb = T["big"], T["bigb"]
        S.dma("pool", T["xT"][:, :, :], self.xscr[t].rearrange("p (a b) -> p a b", a=DC), T["sl_xld"], reads=[self.xscr_b[t]], writes=T["xb"])
        S.dma("pool", big[:, 0:8, :], self.olscr[t].rearrange("p (a b) -> p a b", a=8), T["sl_old"], reads=[self.olscr_b[t]], writes=bigb[0:8])
        S.dma("pool", big[:, 16:20, :], self.xascr[t].rearrange("p (a b) -> p a b", a=4), T["sl_xald"], reads=[self.xascr_b[t]], writes=bigb[16:20])
        wb, w = self.wnext(("mla", j, "uv", 0))
        for h in range(8):
            bk = self.bank("mm")
            S.mm([(ps[:, bk, :], w[:, h * 128:(h + 1) * 128], big[:, h, :], True, True)], reads=[wb, bigb[h]], writes=[pb[bk]])
            S.op("act", lambda e, bk=bk, h=h: e.activation(out=big[:, 8 + h, :], in_=ps[:, bk, :], func=AF.Copy), reads=[pb[bk]], writes=[bigb[8 + h]])
        for k in range(4):
            wb, w = self.wnext(("mla", j, "out", k))
            for dd in range(2):
                dc = 2 * k + dd
                bo = self.bank("acc")
                S.mm([(ps[:, bo, :], w[:, (dd * 12 + kt) * 128:(dd * 12 + kt + 1) * 128], big[:, 8 + kt, :], kt == 0, kt == 11)
                      for kt in range(12)], reads=[wb] + bigb[8:20], writes=[pb[bo]])
                S.op("dve", lambda e, bo=bo, dc=dc: e.tensor_tensor(out=T["xT"][:, dc, :], in0=ps[:, bo, :], in1=T["xT"][:, dc, :], op=ALU.add),
                     reads=[pb[bo], T["xb"][dc]], writes=[T["xb"][dc]])

    def base_tensors(self, A):
        T = {}
        T["xT"] = A.sb("xT", [128, DC, TT], F32)
        T["xb"] = [Buf("x%d" % c) for c in range(DC)]
        T["xh"] = A.sb("xh", [128, DC, TT], BF16)
        T["xhb"] = [Buf("xh%d" % c) for c in range(DC)]
        T["big"] = A.sb("big", [128, FC, TT], BF16)
        T["bigb"] = [Buf("big%d" % c) for c in range(FC)]
        T["sq"] = A.sb("sq", [128, DC, TT], BF16)
        T["sqb"] = [Buf("sq%d" % c) for c in range(DC)]
        T["r"] = A.sb("r", [128, TT], F32)
        T["rb"] = Buf("r")
        T["rtmp"] = A.sb("rtmp", [128, TT], F32)
        T["rtmpb"] = Buf("rtmp")
        T["tmp"] = [A.sb("tmp%d" % i, [128, TT], F32) for i in range(2)]
        T["tmpb"] = [Buf("tmp%d" % i) for i in range(2)]
        T["tmpi"] = 0
        T["pt"] = [A.sb("pt%d" % i, [128, TT], BF16) for i in range(3)]
        T["ptb"] = [Buf("pt%d" % i) for i in range(3)]
        T["pti"] = 0
        return T

    def sgu_tensors(self, A, T, j):
        S = self.S
        T["vtf"] = A.sb("vtf", [128, 4, 1536], F32)
        T["vtfb"] = [[Buf("vtf%d_%d" % (tc, vb)) for vb in range(3)] for tc in range(4)]
        T["vsq"] = A.sb("vsq", [128, 1536], F32)
        T["vsqb"] = Buf("vsq")
        T["ssv"] = A.sb("ssv", [128, 4], F32)
        T["ssvb"] = Buf("ssv")
        T["vh"] = A.sb("vh", [128, 4, 1536], BF16)
        T["vhb"] = [Buf("vh%d" % tc) for tc in range(4)]
        T["gvb"] = A.sb("gvb", [128, 1536], F32)
        T["bsbt"] = A.sb("bsbt", [128, 1024], F32)
        T["sgc"] = Buf("sgc")
        sl = S.slot("sgc%d" % j)
        S.dma("sp", T["gvb"][:, :], self.vnb[j], sl, writes=[T["sgc"]])
        S.dma("sp", T["bsbt"][:, :], self.bsb[j], sl, writes=[T["sgc"]])

    def mlap_tensors(self, A, T, tag):
        S = self.S
        T["cqf"] = A.sb("cqf", [128, 3, TT], F32)
        T["cqfb"] = [Buf("cqf%d" % u) for u in range(3)]
        T["cqh"] = A.sb("cqh", [128, 2, TT], BF16)
        T["cqhb"] = [Buf("cqh%d" % u) for u in range(2)]
        T["ckv"] = A.sb("ckv", [128, TT], BF16)
        T["ckvb"] = Buf("ckv")
        T["kr"] = A.sb("kr", [128, TT], BF16)
        T["krb"] = Buf("kr")
        T["qn"] = [A.sb("qn%d" % i, [128, TT], BF16) for i in range(2)]
        T["qnb"] = [Buf("qn%d" % i) for i in range(2)]
        T["rC"] = A.sb("rC", [128, TT], F32)
        T["rS"] = A.sb("rS", [128, TT], F32)
        T["ropeb"] = Buf("rope")
        T["rt1"] = A.sb("rt1", [128, TT], F32)
        T["rt2"] = A.sb("rt2", [128, TT], F32)
        T["rt1b"], T["rt2b"] = Buf("rt1"), Buf("rt2")
        T["vst"] = A.sb("vst", [128, TT], BF16)
        T["vstb"] = Buf("vst")
        for nm in ("rope", "q", "k0", "k1", "v", "xa", "xst"):
            T["sl_" + nm] = S.slot(tag + nm)

    def phase0(self):
        S, ps, pb = self.S, self.ps, self.pb
        A = Alloc(self.nc, "p0_")
        T = self.base_tensors(A)
        self.start_stream(A, "P0", 1)
        mtok = A.sb("mtok", [128, 2, D], F32)
        mtb = Buf("mtok")
        sl = S.slot("mtok")
        S.dma("pool", mtok[:, :, :], self.mem[:, :].rearrange("(a p) d -> p a d", p=128), sl, writes=[mtb])
        for dc in range(DC):
            bk = self.bank("mm")
            S.tr([(ps[:, bk, a * 128:(a + 1) * 128], mtok[:, a, dc * 128:(dc + 1) * 128], self.identf[:, :]) for a in range(2)],
                 reads=[mtb], writes=[pb[bk]])
            S.op("act", lambda e, bk=bk, dc=dc: e.activation(out=T["xT"][:, dc, 0:NMEM], in_=ps[:, bk, 0:NMEM], func=AF.Copy),
                 reads=[pb[bk]], writes=[T["xb"][dc]])
        for l in range(4):
            self.rmsnorm(T, [(T["xb"][c], T["xT"][:, c, 0:NMEM], 128) for c in range(DC)], G_MEM + l * 8, 1024,
                         [(T["xhb"][c], T["xh"][:, c, 0:NMEM]) for c in range(DC)], N=NMEM)
            wb, w = self.wnext(("mem", l, "k", 0))
            for h in range(4):
                bk = self.bank("mm")
                S.mm([(ps[:, bk, 0:NMEM], w[:, (h * 8 + dc) * 128:(h * 8 + dc + 1) * 128], T["xh"][:, dc, 0:NMEM], dc == 0, dc == 7)
                      for dc in range(DC)], reads=[wb] + T["xhb"], writes=[pb[bk]])
                S.op("act", lambda e, bk=bk, h=h, l=l: e.activation(out=self.KmT[:, l, h, :], in_=ps[:, bk, 0:NMEM], func=AF.Copy),
                     reads=[pb[bk]], writes=[self.kmb[l]])
            wb, w = self.wnext(("mem", l, "v", 0))
            for mc in range(2):
                bk = self.bank("mm")
                S.mm([(ps[:, bk, :], T["xh"][:, dc, mc * 128:(mc + 1) * 128], w[:, dc * 512:(dc + 1) * 512], dc == 0, dc == 7)
                      for dc in range(DC)], reads=[wb] + T["xhb"], writes=[pb[bk]])
                S.op("act", lambda e, bk=bk, mc=mc, l=l: e.activation(out=self.Vm[:, l, mc, :], in_=ps[:, bk, :], func=AF.Copy),
                     reads=[pb[bk]], writes=[self.vmb[l]])
        S.barrier()
        A.close()

    def phaseA(self):
        S, ps, pb = self.S, self.ps, self.pb
        A = Alloc(self.nc, "pA_")
        T = self.base_tensors(A)
        self.start_stream(A, "A", self.NT)
        self.sgu_tensors(A, T, 0)
        self.mlap_tensors(A, T, "A")
        sl_x = S.slot("Ax")
        allv = [b for row in T["vtfb"] for b in row]
        xtok = T["vtf"]
        for t in range(self.NT):
            S.dma("pool", xtok[:, :, 0:D], self.x[t * TT:(t + 1) * TT, :].rearrange("(a p) d -> p a d", p=128), sl_x, writes=allv)
            for dc in range(DC):
                bk = self.bank("mm")
                S.tr([(ps[:, bk, a * 128:(a + 1) * 128], xtok[:, a, dc * 128:(dc + 1) * 128], self.identf[:, :]) for a in range(4)],
                     reads=allv, writes=[pb[bk]])
                S.op("act", lambda e, bk=bk, dc=dc: e.activation(out=T["xT"][:, dc, :], in_=ps[:, bk, :], func=AF.Copy),
                     reads=[pb[bk]], writes=[T["xb"][dc]])
            self.ffn(T, 0, 1)
            self.sgu(T, 0)
            self.ffn(T, 0, 2)
            self.ffn(T, 1, 1)
            self.mla_proj(T, 0, t)
        S.barrier()
        A.close()

    def phaseB2(self):
        S = self.S
        A = Alloc(self.nc, "pB_")
        T = self.base_tensors(A)
        self.start_stream(A, "B2", self.NT)
        self.sgu_tensors(A, T, 1)
        self.mlap_tensors(A, T, "B")
        for nm in ("xld", "old", "xald"):
            T["sl_" + nm] = S.slot("B" + nm)
        for t in range(self.NT):
            self.mla_out(T, 0, t)
            self.ffn(T, 1, 2)
            self.ffn(T, 2, 1)
            self.sgu(T, 1)
            self.ffn(T, 2, 2)
            self.ffn(T, 3, 1)
            self.mla_proj(T, 1, t)
        S.barrier()
        A.close()

    def phaseC2(self):
        S, ps, pb = self.S, self.ps, self.pb
        A = Alloc(self.nc, "pC_")
        T = self.base_tensors(A)
        self.start_stream(A, "C2", self.NT)
        for nm in ("xld", "old", "xald"):
            T["sl_" + nm] = S.slot("C" + nm)
        yT = A.sb("yT", [128, DC, TT], F32)
        yTb = [Buf("yT%d" % c) for c in range(DC)]
        yst = [A.sb("yst%d" % i, [128, D], F32) for i in range(2)]
        ystb = [Buf("yst%d" % i) for i in range(2)]
        sl_y = [S.slot("yst%d" % i) for i in range(2)]
        yb = Buf("y")
        k = 0
        for t in range(self.NT):
            self.mla_out(T, 1, t)
            self.ffn(T, 3, 2)
            self.rmsnorm(T, [(T["xb"][c], T["xT"][:, c, :], 128) for c in range(DC)], G_FIN, 1024,
                         [(yTb[c], yT[:, c, :]) for c in range(DC)])
            for tc in range(4):
                i = k % 2
                k += 1
                for half in range(2):
                    bk = self.bank("mm")
                    S.tr([(ps[:, bk, a * 128:(a + 1) * 128], yT[:, half * 4 + a, tc * 128:(tc + 1) * 128], self.identf[:, :]) for a in range(4)],
                         reads=yTb, writes=[pb[bk]])
                    S.op("act", lambda e, bk=bk, i=i, half=half: e.activation(out=yst[i][:, half * 512:(half + 1) * 512], in_=ps[:, bk, :], func=AF.Copy),
                         reads=[pb[bk]], writes=[ystb[i]])
                S.dma("pool", self.y[t * TT + tc * 128:t * TT + (tc + 1) * 128, :], yst[i][:, :], sl_y[i], reads=[ystb[i]], writes=[yb])
        S.barrier()
        A.close()

    def attention(self, tag):
        S, ps, pb = self.S, self.ps, self.pb
        TOK, NT, NKT = self.TOK, self.NT, self.NKT
        A = Alloc(self.nc, "at%s_" % tag)
        KT0 = A.sb("KT0", [128, self.NK], BF16)
        KT1 = A.sb("KT1", [128, self.NK], BF16)
        Vt = A.sb("Vt", [128, NKT, 128], BF16)
        km = A.sb("km", [128, NKT], F32)
        kb = Buf("K")
        sl = S.slot("atk" + tag)
        exr = [bb for row in self.ex_b for bb in row]
        S.dma("sp", KT0[:, :], self.exin[0:128, :], sl, reads=exr, writes=[kb])
        S.dma("sp", KT1[0:64, :], self.exin[128:192, :], sl, reads=exr, writes=[kb])
        S.dma("sp", Vt[:, :, :].rearrange("p a b -> p (a b)"), self.exin[192:320, :], sl, reads=exr, writes=[kb])
        S.dma("sp", km[:, :], self.kmask[:, :], sl, writes=[kb])
        q = [A.sb("q%d" % i, [128, 16, TT], BF16) for i in range(2)]
        qb = [Buf("q%d" % i) for i in range(2)]
        slq = [S.slot("atq%s%d" % (tag, i)) for i in range(2)]
        ost = [A.sb("ost%d" % i, [128, 8, TT], BF16) for i in range(2)]
        ostb = [[Buf("ost%d_%d" % (i, h)) for h in range(8)] for i in range(2)]
        slo = [S.slot("ato%s%d" % (tag, i)) for i in range(2)]
        NP = 4
        pt = [A.sb("pt%d" % i, [128, TT], BF16) for i in range(NP)]
        ptb = [Buf("apt%d" % i) for i in range(NP)]
        acc = {e: [A.sb("acc%s%d" % (e, i), [128, TT], F32) for i in range(2)] for e in ("dve", "pool")}
        accb = {e: [Buf("acc%s%d" % (e, i)) for i in range(2)] for e in ("dve", "pool")}
        rtmp = A.sb("rtmp", [128, TT], F32)
        rtmpb = Buf("artmp")
        scale = 192.0 ** -0.5

        def loadq(t):
            S.dma("pool", q[t % 2][:, :, :], self.qscr[t].rearrange("p (a b) -> p a b", a=16), slq[t % 2],
                  reads=[self.qscr_b[t]], writes=[qb[t % 2]])

        loadq(0)
        for t in range(NT):
            if t + 1 < NT:
                loadq(t + 1)
            qt, qtb = q[t % 2], qb[t % 2]
            for h in range(8):
                ob, db = 3 + h % 2, 5 + h % 2
                hp = h % 2

                def od(k):
                    S.mm([(ps[:, ob, :], Vt[:, k, :], pt[k % NP][:, :], k == 0, k == NKT - 1)],
                         reads=[ptb[k % NP], kb], writes=[pb[ob]])
                    e = "dve" if k % 2 == 0 else "pool"
                    at, ab = acc[e][hp], accb[e][hp]
                    if k < 2:
                        S.op(e, lambda en, at=at, k=k: en.tensor_copy(out=at[:, :], in_=pt[k % NP][:, :]),
                             reads=[ptb[k % NP]], writes=[ab])
                    else:
                        S.op(e, lambda en, at=at, k=k: en.tensor_tensor(out=at[:, :], in0=at[:, :], in1=pt[k % NP][:, :], op=ALU.add),
                             reads=[ptb[k % NP], ab], writes=[ab])

                for kt in range(NKT):
                    sb_ = kt % 3
                    S.mm([(ps[:, sb_, :], KT0[:, kt * 128:(kt + 1) * 128], qt[:, h, :], True, False),
                          (ps[:, sb_, :], KT1[0:64, kt * 128:(kt + 1) * 128], qt[0:64, 8 + h, :], False, True)],
                         reads=[kb, qtb], writes=[pb[sb_]])
                    S.op("act", lambda e, sb_=sb_, kt=kt: e.activation(out=pt[kt % NP][:, :], in_=ps[:, sb_, :], func=AF.Exp,
                                                                       bias=km[:, kt:kt + 1], scale=scale),
                         reads=[pb[sb_], kb], writes=[ptb[kt % NP]])
                    if kt >= 1:
                        od(kt - 1)
                od(NKT - 1)
                S.mm([(ps[:, db, :], self.onesf[:, :], acc["dve"][hp][:, :], True, False),
                      (ps[:, db, :], self.onesf[:, :], acc["pool"][hp][:, :], False, True)],
                     reads=[accb["dve"][hp], accb["pool"][hp]], writes=[pb[db]])
                S.op("dve", lambda e, db=db: e.reciprocal(out=rtmp[:, :], in_=ps[:, db, :]), reads=[pb[db]], writes=[rtmpb])
                S.op("dve", lambda e, ob=ob, h=h, t=t: e.tensor_tensor(out=ost[t % 2][:, h, :], in0=ps[:, ob, :], in1=rtmp[:, :], op=ALU.mult),
                     reads=[pb[ob], rtmpb], writes=[ostb[t % 2][h]])
            S.dma("pool", self.olscr[t].rearrange("p (a b) -> p a b", a=8), ost[t % 2][:, :, :], slo[t % 2],
                  reads=ostb[t % 2], writes=[self.olscr_b[t]])
        S.barrier()
        A.close()

    def build(self):
        import os
        dbg = os.environ.get("KDEBUG", "")
        self.setup()
        if dbg == "setup":
            return self.nc
        self.convert()
        if dbg == "convert":
            return self.nc
        self.phase0()
        if dbg == "p0":
            return self.nc
        self.phaseA()
        if dbg == "A":
            return self.nc
        self.attention("1")
        self.phaseB2()
        self.attention("2")
        self.phaseC2()
        return self.nc


def _block_table():
    t = []
    for which in (1, 2):
        for l in range(4):
            t += [(("ffn", l, which, "in", k), 4096) for k in range(11)]
            t += [(("ffn", l, which, "out", dc), 2816) for dc in range(8)]
    for j in range(2):
        t += [(("sgu", j, "u", g2), 3072) for g2 in range(4)]
        t += [(("sgu", j, "v", vb), 4096) for vb in range(3)]
        t += [(("sgu", j, "xaq", 0), 4096), (("sgu", j, "ws", 0), 1024)]
        t += [(("sgu", j, "out", dc), 2560) for dc in range(8)]
        t += [(("mla", j, "in", 0), 4096), (("mla", j, "in", 1), 4096), (("mla", j, "uq", 0), 4096),
              (("mla", j, "uk", 0), 1024), (("mla", j, "uv", 0), 1024)]
        t += [(("mla", j, "out", k), 3072) for k in range(4)]
    for l in range(4):
        t += [(("mem", l, "k", 0), 4096), (("mem", l, "v", 0), 4096)]
    return t


NBLK = len(_block_table())


def _fm(wcols):
    k = wcols.shape[0] // 128
    return wcols.reshape(k, 128, wcols.shape[1]).transpose(1, 0, 2)


def _block_image(name, p, img):
    perm = np.concatenate([np.arange(32, 64), np.arange(0, 32)])
    kind = name[0]
    if kind == "ffn":
        _, l, which, io, k = name
        ffw = {1: (p["ffn1_w_in"], p["ffn1_w_out"]), 2: (p["ffn2_w_in"], p["ffn2_w_out"])}
        win, wout = ffw[which][0][l], ffw[which][1][l]
        if io == "in":
            for jj in range(2):
                j = 2 * k + jj
                V = img[:, jj * 2048:(jj + 1) * 2048].reshape(128, 8, 256)
                V[:, :, 0:128] = _fm(win[:, j * 128:(j + 1) * 128])
                V[:, :, 128:256] = _fm(win[:, DFF + j * 128:DFF + (j + 1) * 128])
        else:
            img[:, :2816].reshape(128, 22, 128)[:] = _fm(wout[:, k * 128:(k + 1) * 128])
    elif kind == "sgu":
        _, j, what, k = name
        win, wout = p["sgu_w_in"][j], p["sgu_w_out"][j]
        if what == "u":
            for gg in range(2):
                g = 2 * k + gg
                img[:, gg * 1536:(gg + 1) * 1536].reshape(128, 8, 192)[:] = _fm(win[:, 192 * g:192 * g + 192])
        elif what == "v":
            img.reshape(128, 8, 512)[:] = _fm(win[:, 1536 + 512 * k:1536 + 512 * k + 512])
        elif what == "xaq":
            for h in range(4):
                img[:, h * 1024:(h + 1) * 1024].reshape(128, 8, 128)[:] = _fm(win[:, 3072 + 128 * h:3072 + 128 * h + 128])
        elif what == "ws":
            img[:, :1024].reshape(128, 8, 128)[:] = np.transpose(p["sgu_w_s"][j], (2, 0, 1))
        else:
            Wc = wout[:, k * 128:(k + 1) * 128]
            gv = Wc[:1536].reshape(8, 192, 128)
            img[:, 0:1024].reshape(128, 8, 128)[:] = gv[:, :128].transpose(1, 0, 2)
            img[0:64, 1024:2048].reshape(64, 8, 128)[:] = gv[:, 128:192].transpose(1, 0, 2)
            img[:, 2048:2560].reshape(128, 4, 128)[:] = _fm(Wc[1536:])
    elif kind == "mla":
        _, j, what, k = name
        win = p["mla_w_in"][j]
        if what == "in" and k == 0:
            for u in range(3):
                img[:, u * 1024:(u + 1) * 1024].reshape(128, 8, 128)[:] = _fm(win[:, u * 128:(u + 1) * 128])
            V = img[:, 3072:4096].reshape(128, 8, 128)
            V[:, :, 0:64] = _fm(win[:, 384:448])
            V[:, :, 64:128] = _fm(win[:, 384 + perm])
        elif what == "in":
            for h in range(4):
                img[:, h * 1024:(h + 1) * 1024].reshape(128, 8, 128)[:] = _fm(win[:, 448 + 128 * h:448 + 128 * h + 128])
        elif what == "uq":
            wq = p["mla_w_uq"][j].reshape(2, 128, 8, 192)
            V = img.reshape(128, 8, 2, 256)
            V[..., 0:192] = wq.transpose(1, 2, 0, 3)
            V[..., 192:256] = wq[..., 128 + perm].transpose(1, 2, 0, 3)
        elif what == "uk":
            img[:, :1024].reshape(128, 8, 128)[:] = np.transpose(p["mla_w_uk"][j], (2, 1, 0))
        elif what == "uv":
            img[:, :1024].reshape(128, 8, 128)[:] = p["mla_w_uv"][j]
        else:
            for dd in range(2):
                dc = 2 * k + dd
                img[:, dd * 1536:(dd + 1) * 1536].reshape(128, 12, 128)[:] = _fm(p["mla_w_out"][j][:, dc * 128:(dc + 1) * 128])
    else:
        _, l, what, _k = name
        wm = p["w_mem_kv"][l]
        if what == "k":
            for h in range(4):
                img[:, h * 1024:(h + 1) * 1024].reshape(128, 8, 128)[:] = _fm(wm[:, h * 128:(h + 1) * 128])
        else:
            img.reshape(128, 8, 512)[:] = _fm(wm[:, 512:1024])


def _weight_images(p):
    out = np.zeros((NBLK, 128, WBLK), np.float32)
    for bid, (name, n) in enumerate(_block_table()):
        _block_image(name, p, out[bid])
    return out


def _rope_tables(pos0, n):
    inv = (1.0 / (np.float32(10000.0) ** (np.arange(0, 64, 2, dtype=np.float32) / np.float32(64)))).astype(np.float32)
    ang = (np.arange(pos0, pos0 + n, dtype=np.float32)[:, None] * inv[None, :]).astype(np.float32)
    c, s = np.cos(ang).astype(np.float32).T, np.sin(ang).astype(np.float32).T
    C = np.concatenate([c, c], 0)
    Sg = np.concatenate([-s, s], 0)
    return np.ascontiguousarray(C), np.ascontiguousarray(Sg)


def _prep_shared(p):
    f = lambda a: np.ascontiguousarray(np.asarray(a, dtype=np.float32))
    sh = {}
    gp = np.zeros((128, NG), np.float32)

    def put(col, vec):
        v = np.asarray(vec, np.float32).reshape(-1, 128)
        gp[:, col:col + v.shape[0]] = v.T

    for l in range(4):
        put(G_FFN1 + 8 * l, p["ffn1_norm"][l])
        put(G_MIX + 8 * l, p["mix_norm"][l])
        put(G_MEM + 8 * l, p["mem_norm"][l])
        put(G_FFN2 + 8 * l, p["ffn2_norm"][l])
    put(G_FIN, p["final_norm"])
    for j in range(2):
        put(G_QN + 2 * j, p["mla_q_norm"][j])
        put(G_KVN + j, p["mla_kv_norm"][j])
    sh["gpack"] = gp
    sh["vnb"] = f(np.broadcast_to(np.asarray(p["sgu_v_norm"], np.float32)[:, None, :], (2, 128, 1536)))
    sh["bsb"] = f(np.broadcast_to(np.asarray(p["sgu_b_s"], np.float32).reshape(2, 1, 1024), (2, 128, 1024)))
    sh["ident"] = np.eye(128, dtype=np.float32)
    return sh


_NC_CACHE = {}


def run(p, x_prompt, x_sample, mem_prompt, mem_sample):
    x_prompt, x_sample = np.asarray(x_prompt, np.float32), np.asarray(x_sample, np.float32)
    mem_prompt, mem_sample = np.asarray(mem_prompt, np.float32), np.asarray(mem_sample, np.float32)
    SP, TOK = x_prompt.shape[1], x_sample.shape[1]
    assert x_prompt.shape[0] == 4 and x_sample.shape[0] == 2 and SP <= TOK and TOK % TT == 0 and SP % 128 == 0
    NKT = TOK // 128
    p = {k: np.asarray(v, np.float32) for k, v in p.items()}
    sh = _prep_shared(p)
    sh["wsrc"] = _weight_images(p)
    ropeC, ropeS = _rope_tables(0, TOK)
    in_maps = []
    for c in range(8):
        m = dict(sh)
        xx = np.zeros((TOK, D), np.float32)
        km = np.zeros((128, NKT), np.float32)
        if c < 4:
            xx[:SP] = x_prompt[c]
            m["mem"] = np.ascontiguousarray(mem_prompt[c])
            km[:, SP // 128:] = NEG
        elif c < 6:
            xx[:] = x_sample[c - 4]
            m["mem"] = np.ascontiguousarray(mem_sample[c - 4])
        else:
            m["mem"] = np.ascontiguousarray(mem_sample[0])
        m["x"], m["kmask"], m["ropeC"], m["ropeS"] = xx, km, ropeC, ropeS
        in_maps.append(m)
    if TOK not in _NC_CACHE:
        _NC_CACHE[TOK] = Builder(TOK).build()
    nc = _NC_CACHE[TOK]
    res = run_bass_kernel_spmd(nc, in_maps, core_ids=list(range(8)))
    ys = [np.asarray(res.results[c]["y"], np.float32) for c in range(6)]
    y_prompt = np.stack([ys[c][:SP] for c in range(4)], 0)
    y_sample = np.stack(ys[4:6], 0)
    return y_prompt, y_sample


def kernel(x_prompt, x_sample, mem_prompt, mem_sample, **p):
    return run(p, x_prompt, x_sample, mem_prompt, mem_sample)
```

```python
from contextlib import ExitStack
import numpy as np
import concourse.bass as bass
import concourse.mybir as mybir
from concourse.bass_utils import run_bass_kernel_spmd

F32 = mybir.dt.float32
BF16 = mybir.dt.bfloat16
AF = mybir.ActivationFunctionType
ALU = mybir.AluOpType
AX = mybir.AxisListType

D = 1024
DC = 8
DFF = 2816
FC = 22
TT = 512
NMEM = 256
EPS = 1e-6
NSLOT = 5
WBLK = 4096
G_FFN1, G_MIX, G_MEM, G_FFN2, G_FIN, G_QN, G_KVN = 0, 32, 64, 96, 128, 136, 140
NG = 142
NEG = -30000.0


class Buf:
    __slots__ = ("name", "w", "r")

    def __init__(self, name):
        self.name = name
        self.w = {}
        self.r = {}


class Sched:
    def __init__(self, nc):
        self.nc = nc
        self.eng = {"pe": nc.tensor, "act": nc.scalar, "dve": nc.vector, "pool": nc.gpsimd, "sp": nc.sync}
        self.sem = {}
        self.cnt = {}
        self.seen = {e: {} for e in self.eng}
        for e in ("pe", "act", "dve", "pool"):
            self.sem[e] = nc.alloc_semaphore("s_" + e)
            self.cnt[e] = 0

    def slot(self, name):
        key = "d_" + name
        self.sem[key] = self.nc.alloc_semaphore(key)
        self.cnt[key] = 0
        return key

    def _wait(self, eng, deps, skip=None):
        for k, v in deps.items():
            if k == skip or v <= 0:
                continue
            if eng == "pe" and k == "pe":
                continue
            if self.seen[eng].get(k, 0) >= v:
                continue
            self.eng[eng].wait_ge(self.sem[k], v)
            self.seen[eng][k] = v

    @staticmethod
    def _deps(reads, writes):
        d = {}
        for b in reads:
            for k, v in b.w.items():
                if d.get(k, 0) < v:
                    d[k] = v
        for b in writes:
            for k, v in b.w.items():
                if d.get(k, 0) < v:
                    d[k] = v
            for k, v in b.r.items():
                if d.get(k, 0) < v:
                    d[k] = v
        return d

    def _mark(self, key, v, reads, writes):
        for b in reads:
            b.r[key] = v
        for b in writes:
            b.w = {key: v}
            b.r = {}

    def op(self, eng, fn, reads=(), writes=()):
        self._wait(eng, self._deps(reads, writes))
        ins = fn(self.eng[eng])
        self.cnt[eng] += 1
        ins.then_inc(self.sem[eng], 1)
        self._mark(eng, self.cnt[eng], reads, writes)

    def mm(self, items, reads=(), writes=()):
        self._wait("pe", self._deps(reads, writes))
        ins = None
        for (o, l, r, st, sp_) in items:
            ins = self.nc.tensor.matmul(o, lhsT=l, rhs=r, start=st, stop=sp_)
        self.cnt["pe"] += 1
        ins.then_inc(self.sem["pe"], 1)
        self._mark("pe", self.cnt["pe"], reads, writes)

    def tr(self, items, reads=(), writes=()):
        self._wait("pe", self._deps(reads, writes))
        ins = None
        for (o, i, idn) in items:
            ins = self.nc.tensor.transpose(o, i, idn)
        self.cnt["pe"] += 1
        ins.then_inc(self.sem["pe"], 1)
        self._mark("pe", self.cnt["pe"], reads, writes)

    def dma(self, q, out, in_, slot, reads=(), writes=()):
        self._wait(q, self._deps(reads, writes), skip=slot)
        self.eng[q].dma_start(out=out, in_=in_).then_inc(self.sem[slot], 16)
        self.cnt[slot] += 16
        self._mark(slot, self.cnt[slot], reads, writes)

    def barrier(self):
        for e in self.eng:
            self._wait(e, dict(self.cnt))


class Alloc:
    def __init__(self, nc, prefix):
        self.nc = nc
        self.prefix = prefix
        self.es = ExitStack()

    def sb(self, name, shape, dt):
        return self.es.enter_context(self.nc.sbuf_tensor(self.prefix + name, list(shape), dt))

    def close(self):
        self.es.close()


def _stream_names(tag):
    def ffn(l, w):
        return [("ffn", l, w, "in", k) for k in range(11)] + [("ffn", l, w, "out", dc) for dc in range(8)]

    def sgu(j):
        return ([("sgu", j, "v", vb) for vb in range(3)] + [("sgu", j, "u", g2) for g2 in range(4)]
                + [("sgu", j, "xaq", 0), ("sgu", j, "ws", 0)] + [("sgu", j, "out", dc) for dc in range(8)])

    def mlap(j):
        return [("mla", j, "in", 0), ("mla", j, "in", 1), ("mla", j, "uq", 0), ("mla", j, "uk", 0)]

    def mlao(j):
        return [("mla", j, "uv", 0)] + [("mla", j, "out", k) for k in range(4)]

    if tag == "P0":
        return [("mem", l, kv, 0) for l in range(4) for kv in ("k", "v")]
    if tag == "A":
        return ffn(0, 1) + sgu(0) + ffn(0, 2) + ffn(1, 1) + mlap(0)
    if tag == "B2":
        return mlao(0) + ffn(1, 2) + ffn(2, 1) + sgu(1) + ffn(2, 2) + ffn(3, 1) + mlap(1)
    if tag == "C2":
        return mlao(1) + ffn(3, 2)
    raise ValueError(tag)


class Builder:
    def __init__(self, TOK):
        self.TOK = TOK
        self.NT = TOK // TT
        self.NK = TOK
        self.NKT = self.NK // 128
        nc = bass.Bass("TRN2", target_bir_lowering=False)
        self.nc = nc
        self.S = Sched(nc)
        self.declare()
        self.plan_blocks()

    def declare(self):
        nc, TOK, NT = self.nc, self.TOK, self.NT

        def inp(name, shape, dt=F32):
            return nc.dram_tensor(name, list(shape), dt, kind="ExternalInput")

        self.x = inp("x", [TOK, D])
        self.mem = inp("mem", [NMEM, D])
        self.ropeC = inp("ropeC", [64, TOK])
        self.ropeS = inp("ropeS", [64, TOK])
        self.kmask = inp("kmask", [128, self.NKT])
        self.ident = inp("ident", [128, 128])
        self.gpack = inp("gpack", [128, NG])
        self.vnb = inp("vnb", [2, 128, 1536])
        self.bsb = inp("bsb", [2, 128, 1024])
        self.y = nc.dram_tensor("y", [TOK, D], F32, kind="ExternalOutput")

        def scr(name, shape, dt):
            return nc.dram_tensor(name, list(shape), dt, kind="Internal")

        self.xscr = scr("xscr", [NT, 128, DC * TT], F32)
        self.qscr = scr("qscr", [NT, 128, 16 * TT], BF16)
        self.xascr = scr("xascr", [NT, 128, 4 * TT], BF16)
        self.olscr = scr("olscr", [NT, 128, 8 * TT], BF16)
        self.exin = scr("exin", [320, TOK], BF16)
        self.xscr_b = [Buf("xscr%d" % t) for t in range(NT)]
        self.qscr_b = [Buf("qscr%d" % t) for t in range(NT)]
        self.xascr_b = [Buf("xascr%d" % t) for t in range(NT)]
        self.olscr_b = [Buf("olscr%d" % t) for t in range(NT)]
        self.ex_b = [[Buf("ex%d_%d" % (t, i)) for i in range(3)] for t in range(NT)]

    def plan_blocks(self):
        self.blocks = {nm: (i, n) for i, (nm, n) in enumerate(_block_table())}
        self.wsrc = self.nc.dram_tensor("wsrc", [NBLK, 128, WBLK], F32, kind="ExternalInput")
        self.wscr = self.nc.dram_tensor("wscr", [NBLK, 128, WBLK], BF16, kind="Internal")
        self.wscr_b = Buf("wscr")

    def wblk(self, bid, n):
        return self.wscr[bid, :, 0:n]

    def setup(self):
        nc, S = self.nc, self.S
        G = Alloc(nc, "g_")
        self.G = G
        self.gp = G.sb("gp", [128, NG], F32)
        self.identf = G.sb("identf", [128, 128], F32)
        self.identb = G.sb("identb", [128, 128], BF16)
        self.ones = {n: G.sb("ones%d" % n, [128, 128], BF16) for n in (1, 128, 256, 1024)}
        self.onesf = G.sb("onesf", [128, 128], F32)
        self.KmT = G.sb("KmT", [128, 4, 4, NMEM], BF16)
        self.Vm = G.sb("Vm", [128, 4, 2, 512], BF16)
        self.kmb = [Buf("kmT%d" % l) for l in range(4)]
        self.vmb = [Buf("vm%d" % l) for l in range(4)]
        self.ps = nc.alloc_psum_tensor("ps", [128, 7, 512], F32)
        self.psb = nc.alloc_psum_tensor("psb", [128, 1024], BF16)
        self.pb = [Buf("ps%d" % i) for i in range(7)]
        self.psbb = Buf("psb")
        self.rr = {"mm": [0, 1, 2, 3], "acc": [4, 5]}
        self.rri = {"mm": 0, "acc": 0}
        self.cb = Buf("consts")
        sl = S.slot("const")
        S.dma("sp", self.gp[:, :], self.gpack[:, :], sl, writes=[self.cb])
        S.dma("sp", self.identf[:, :], self.ident[:, :], sl, writes=[self.cb])
        S.op("dve", lambda e: e.tensor_copy(out=self.identb[:, :], in_=self.identf[:, :]), reads=[self.cb], writes=[self.cb])
        for n, t in self.ones.items():
            S.op("pool", lambda e, t=t, n=n: e.memset(t[:, :], 1.0 / n), writes=[self.cb])
        S.op("pool", lambda e: e.memset(self.onesf[:, :], 1.0), writes=[self.cb])
        S.barrier()

    def bank(self, grp):
        b = self.rr[grp][self.rri[grp] % len(self.rr[grp])]
        self.rri[grp] += 1
        return b

    def start_stream(self, A, tag, reps):
        self.wslot_t = [A.sb("wslot%d" % i, [128, WBLK], BF16) for i in range(NSLOT)]
        self.wslot_b = [Buf("wslot%d" % i) for i in range(NSLOT)]
        if not hasattr(self, "wslot_s"):
            self.wslot_s = [self.S.slot("wslot%d" % i) for i in range(NSLOT)]
        self.seq = _stream_names(tag) * reps
        self.seq_i = 0
        self.seq_loaded = 0

    def wnext(self, name):
        assert self.seq[self.seq_i] == name, (self.seq[self.seq_i], name)
        i = self.seq_i
        while self.seq_loaded < min(len(self.seq), i + NSLOT - 1):
            k = self.seq_loaded
            bid, n = self.blocks[self.seq[k]]
            s = k % NSLOT
            self.S.dma("sp", self.wslot_t[s][:, 0:n], self.wblk(bid, n), self.wslot_s[s],
                       reads=[self.wscr_b], writes=[self.wslot_b[s]])
            self.seq_loaded += 1
        self.seq_i += 1
        return self.wslot_b[i % NSLOT], self.wslot_t[i % NSLOT]

    def convert(self):
        S = self.S
        A = Alloc(self.nc, "cv_")
        stf = [A.sb("stf%d" % i, [128, WBLK], F32) for i in range(2)]
        stb = [A.sb("stb%d" % i, [128, WBLK], BF16) for i in range(2)]
        stfb = [Buf("stf%d" % i) for i in range(2)]
        stbb = [Buf("stb%d" % i) for i in range(2)]
        sl_in = [S.slot("cvi%d" % i) for i in range(2)]
        sl_out = [S.slot("cvo%d" % i) for i in range(2)]
        engs = ["dve", "act", "pool"]
        for bid in range(NBLK):
            s = bid % 2
            S.dma("sp", stf[s][:, :], self.wsrc[bid], sl_in[s], writes=[stfb[s]])
            e = engs[bid % 3]
            if e == "act":
                S.op("act", lambda en, s=s: en.activation(out=stb[s][:, :], in_=stf[s][:, :], func=AF.Copy),
                     reads=[stfb[s]], writes=[stbb[s]])
            else:
                S.op(e, lambda en, s=s: en.tensor_copy(out=stb[s][:, :], in_=stf[s][:, :]), reads=[stfb[s]], writes=[stbb[s]])
            S.dma("pool", self.wscr[bid], stb[s][:, :], sl_out[s], reads=[stbb[s]])
        self.wscr_b.w = {sl_out[0]: S.cnt[sl_out[0]], sl_out[1]: S.cnt[sl_out[1]]}
        S.barrier()
        A.close()

    def rmsnorm(self, T, srcs, gcol, nfeat, outs, N=TT):
        S, ps = self.S, self.ps
        n = len(srcs)
        for c, (b, ap, rows) in enumerate(srcs):
            S.op("act", lambda e, c=c, ap=ap, rows=rows: e.activation(out=T["sq"][0:rows, c, 0:N], in_=ap, func=AF.Square),
                 reads=[b], writes=[T["sqb"][c]])
        S.mm([(ps[:, 6, 0:N], self.ones[nfeat][0:srcs[c][2], :], T["sq"][0:srcs[c][2], c, 0:N], c == 0, c == n - 1) for c in range(n)],
             reads=T["sqb"][:n], writes=[self.pb[6]])
        S.op("act", lambda e: e.activation(out=T["r"][:, 0:N], in_=ps[:, 6, 0:N], func=AF.Sqrt, bias=EPS, scale=1.0),
             reads=[self.pb[6]], writes=[T["rb"]])
        S.op("dve", lambda e: e.reciprocal(out=T["r"][:, 0:N], in_=T["r"][:, 0:N]), reads=[T["rb"]], writes=[T["rb"]])
        for c, (b, ap, rows) in enumerate(srcs):
            ob, oap = outs[c]
            S.op("dve", lambda e, c=c, ap=ap, rows=rows, oap=oap: e.scalar_tensor_tensor(
                out=oap, in0=ap, scalar=self.gp[0:rows, gcol + c:gcol + c + 1], in1=T["r"][0:rows, 0:N],
                op0=ALU.mult, op1=ALU.mult), reads=[b, T["rb"]], writes=[ob])

    def xnorm(self, T, gcol):
        self.rmsnorm(T, [(T["xb"][c], T["xT"][:, c, :], 128) for c in range(DC)], gcol, 1024,
                     [(T["xhb"][c], T["xh"][:, c, :]) for c in range(DC)])

    def tmpf(self, T):
        i = T["tmpi"] % 2
        T["tmpi"] += 1
        return T["tmp"][i], T["tmpb"][i]

    def pbuf(self, T):
        i = T["pti"] % 3
        T["pti"] += 1
        return T["pt"][i], T["ptb"][i]

    def ffn(self, T, l, which):
        S, ps, pb = self.S, self.ps, self.pb
        big, bigb, xh, xhb = T["big"], T["bigb"], T["xh"], T["xhb"]
        self.xnorm(T, (G_FFN1 if which == 1 else G_FFN2) + l * 8)
        for k in range(11):
            wb, w = self.wnext(("ffn", l, which, "in", k))
            for jj in range(2):
                j = 2 * k + jj
                bg, bu = self.bank("mm"), self.bank("mm")
                for half, bk in ((0, bg), (1, bu)):
                    S.mm([(ps[:, bk, :], w[:, (jj * 8 + dc) * 256 + half * 128:(jj * 8 + dc) * 256 + half * 128 + 128],
                           xh[:, dc, :], dc == 0, dc == 7) for dc in range(DC)], reads=[wb] + xhb, writes=[pb[bk]])
                tp, tb = self.tmpf(T)
                S.op("act", lambda e, bg=bg, tp=tp: e.activation(out=tp[:, :], in_=ps[:, bg, :], func=AF.Silu),
                     reads=[pb[bg]], writes=[tb])
                S.op("dve", lambda e, bu=bu, tp=tp, j=j: e.tensor_tensor(out=big[:, j, :], in0=tp[:, :], in1=ps[:, bu, :], op=ALU.mult),
                     reads=[tb, pb[bu]], writes=[bigb[j]])
        for dc in range(DC):
            wb, w = self.wnext(("ffn", l, which, "out", dc))
            bo = self.bank("acc")
            S.mm([(ps[:, bo, :], w[:, j * 128:(j + 1) * 128], big[:, j, :], j == 0, j == FC - 1) for j in range(FC)],
                 reads=[wb] + bigb[:FC], writes=[pb[bo]])
            S.op("dve", lambda e, bo=bo, dc=dc: e.scalar_tensor_tensor(
                out=T["xT"][:, dc, :], in0=ps[:, bo, :], scalar=0.5, in1=T["xT"][:, dc, :], op0=ALU.mult, op1=ALU.add),
                reads=[pb[bo], T["xb"][dc]], writes=[T["xb"][dc]])

    def xa_attn(self, T, l):
        S, ps, pb = self.S, self.ps, self.pb
        big, bigb = T["big"], T["bigb"]
        for h in range(4):
            qi = 16 + h
            pts = []
            for mc in range(2):
                sbk = self.bank("mm")
                S.mm([(ps[:, sbk, :], self.KmT[:, l, h, mc * 128:(mc + 1) * 128], big[:, qi, :], True, True)],
                     reads=[bigb[qi], self.kmb[l]], writes=[pb[sbk]])
                pt, ptb = self.pbuf(T)
                S.op("act", lambda e, sbk=sbk, pt=pt: e.activation(out=pt[:, :], in_=ps[:, sbk, :], func=AF.Exp, scale=128.0 ** -0.5),
                     reads=[pb[sbk]], writes=[ptb])
                pts.append((pt, ptb))
            bo = self.bank("acc")
            S.mm([(ps[:, bo, :], self.Vm[:, l, mc, h * 128:(h + 1) * 128], pts[mc][0][:, :], mc == 0, mc == 1) for mc in range(2)]
                 + [(ps[:, 6, :], self.ones[1][:, :], pts[mc][0][:, :], mc == 0, mc == 1) for mc in range(2)],
                 reads=[pts[0][1], pts[1][1], self.vmb[l]], writes=[pb[bo], pb[6]])
            S.op("dve", lambda e: e.reciprocal(out=T["rtmp"][:, :], in_=ps[:, 6, :]), reads=[pb[6]], writes=[T["rtmpb"]])
            S.op("dve", lambda e, bo=bo, qi=qi: e.tensor_tensor(out=big[:, qi, :], in0=ps[:, bo, :], in1=T["rtmp"][:, :], op=ALU.mult),
                 reads=[pb[bo], T["rtmpb"]], writes=[bigb[qi]])

    def sgu(self, T, j):
        S, ps, pb = self.S, self.ps, self.pb
        big, bigb, xh, xhb = T["big"], T["bigb"], T["xh"], T["xhb"]
        l = 2 * j
        self.xnorm(T, G_MIX + l * 8)
        vtf, vtfb, vh, vhb = T["vtf"], T["vtfb"], T["vh"], T["vhb"]
        for vb in range(3):
            wb, w = self.wnext(("sgu", j, "v", vb))
            for tc in range(4):
                bk = self.bank("mm")
                S.mm([(ps[:, bk, :], xh[:, dc, tc * 128:(tc + 1) * 128], w[:, dc * 512:(dc + 1) * 512], dc == 0, dc == 7)
                      for dc in range(DC)], reads=[wb] + xhb, writes=[pb[bk]])
                S.op("act", lambda e, bk=bk, tc=tc, vb=vb: e.activation(out=vtf[:, tc, vb * 512:(vb + 1) * 512], in_=ps[:, bk, :],
                                                                       func=AF.Gelu_apprx_tanh), reads=[pb[bk]], writes=[vtfb[tc][vb]])
        for tc in range(4):
            S.op("act", lambda e, tc=tc: e.activation(out=T["vsq"][:, :], in_=vtf[:, tc, :], func=AF.Square),
                 reads=vtfb[tc], writes=[T["vsqb"]])
            S.op("dve", lambda e, tc=tc: e.reduce_sum(out=T["ssv"][:, tc:tc + 1], in_=T["vsq"][:, :], axis=AX.X),
                 reads=[T["vsqb"]], writes=[T["ssvb"]])
            S.op("act", lambda e, tc=tc: e.activation(out=T["ssv"][:, tc:tc + 1], in_=T["ssv"][:, tc:tc + 1], func=AF.Sqrt,
                                                      bias=EPS, scale=1.0 / 1536), reads=[T["ssvb"]], writes=[T["ssvb"]])
            S.op("dve", lambda e, tc=tc: e.reciprocal(out=T["ssv"][:, tc:tc + 1], in_=T["ssv"][:, tc:tc + 1]),
                 reads=[T["ssvb"]], writes=[T["ssvb"]])
            S.op("dve", lambda e, tc=tc: e.scalar_tensor_tensor(out=vh[:, tc, :], in0=vtf[:, tc, :], scalar=T["ssv"][:, tc:tc + 1],
                                                                in1=T["gvb"][:, :], op0=ALU.mult, op1=ALU.mult),
                 reads=vtfb[tc] + [T["ssvb"], T["sgc"]], writes=[vhb[tc]])
        for g2 in range(4):
            wb, w = self.wnext(("sgu", j, "u", g2))
            for gg in range(2):
                g = 2 * g2 + gg
                for (ti, rows, off) in ((g, 128, 0), (8 + g, 64, 128)):
                    bk = self.bank("mm")
                    S.mm([(ps[0:rows, bk, :], w[:, (gg * 8 + dc) * 192 + off:(gg * 8 + dc) * 192 + off + rows], xh[:, dc, :],
                           dc == 0, dc == 7) for dc in range(DC)], reads=[wb] + xhb, writes=[pb[bk]])
                    S.op("act", lambda e, bk=bk, ti=ti, rows=rows: e.activation(out=big[0:rows, ti, :], in_=ps[0:rows, bk, :],
                                                                                func=AF.Gelu_apprx_tanh), reads=[pb[bk]], writes=[bigb[ti]])
        wb, w = self.wnext(("sgu", j, "xaq", 0))
        for h in range(4):
            bk = self.bank("mm")
            S.mm([(ps[:, bk, :], w[:, (h * 8 + dc) * 128:(h * 8 + dc + 1) * 128], xh[:, dc, :], dc == 0, dc == 7) for dc in range(DC)],
                 reads=[wb] + xhb, writes=[pb[bk]])
            S.op("dve", lambda e, bk=bk, h=h: e.tensor_copy(out=big[:, 16 + h, :], in_=ps[:, bk, :]), reads=[pb[bk]], writes=[bigb[16 + h]])
        wb, w = self.wnext(("sgu", j, "ws", 0))
        for g in range(8):
            for (ti, rows, off) in ((g, 128, 0), (8 + g, 64, 128)):
                bk = self.bank("mm")
                S.mm([(ps[0:rows, bk, tc * 128:(tc + 1) * 128], vh[:, tc, 192 * g + off:192 * g + off + rows],
                       w[:, g * 128:(g + 1) * 128], True, True) for tc in range(4)], reads=[wb] + vhb, writes=[pb[bk]])
                tp, tb = self.tmpf(T)
                bias = T["bsbt"][0:rows, g * 128:(g + 1) * 128].unsqueeze(1).broadcast_to([rows, 4, 128])
                S.op("dve", lambda e, bk=bk, tp=tp, rows=rows, bias=bias: e.tensor_tensor(
                    out=tp[0:rows, :].rearrange("p (a b) -> p a b", a=4), in0=ps[0:rows, bk, :].rearrange("p (a b) -> p a b", a=4),
                    in1=bias, op=ALU.add), reads=[pb[bk], T["sgc"]], writes=[tb])
                S.op("dve", lambda e, tp=tp, ti=ti, rows=rows: e.tensor_tensor(out=big[0:rows, ti, :], in0=tp[0:rows, :],
                                                                               in1=big[0:rows, ti, :], op=ALU.mult),
                     reads=[tb, bigb[ti]], writes=[bigb[ti]])
        self.xa_attn(T, l)
        for dc in range(DC):
            wb, w = self.wnext(("sgu", j, "out", dc))
            bo = self.bank("acc")
            items = []
            for t in range(20):
                rows = 64 if 8 <= t < 16 else 128
                items.append((ps[:, bo, :], w[0:rows, t * 128:(t + 1) * 128], big[0:rows, t, :], t == 0, t == 19))
            S.mm(items, reads=[wb] + bigb[:20], writes=[pb[bo]])
            S.op("dve", lambda e, bo=bo, dc=dc: e.tensor_tensor(out=T["xT"][:, dc, :], in0=ps[:, bo, :], in1=T["xT"][:, dc, :], op=ALU.add),
                 reads=[pb[bo], T["xb"][dc]], writes=[T["xb"][dc]])

    def rope(self, T, ba, bb_, out_ap, out_b):
        S, ps, pb = self.S, self.ps, self.pb
        S.op("dve", lambda e: e.tensor_tensor(out=T["rt1"][0:64, :], in0=ps[0:64, ba, :], in1=T["rC"][0:64, :], op=ALU.mult),
             reads=[pb[ba], T["ropeb"]], writes=[T["rt1b"]])
        S.op("dve", lambda e: e.tensor_tensor(out=T["rt2"][0:64, :], in0=ps[0:64, bb_, :], in1=T["rS"][0:64, :], op=ALU.mult),
             reads=[pb[bb_], T["ropeb"]], writes=[T["rt2b"]])
        S.op("pool", lambda e: e.tensor_tensor(out=out_ap, in0=T["rt1"][0:64, :], in1=T["rt2"][0:64, :], op=ALU.add),
             reads=[T["rt1b"], T["rt2b"]], writes=[out_b])

    def mla_proj(self, T, j, t):
        S, ps, pb = self.S, self.ps, self.pb
        big, bigb, xh, xhb = T["big"], T["bigb"], T["xh"], T["xhb"]
        l = 2 * j + 1
        cols = slice(t * TT, (t + 1) * TT)
        S.dma("pool", T["rC"][0:64, :], self.ropeC[:, cols], T["sl_rope"], writes=[T["ropeb"]])
        S.dma("pool", T["rS"][0:64, :], self.ropeS[:, cols], T["sl_rope"], writes=[T["ropeb"]])
        self.xnorm(T, G_MIX + l * 8)
        wb, w = self.wnext(("mla", j, "in", 0))
        cqf, cqfb = T["cqf"], T["cqfb"]
        for u in range(3):
            bk = self.bank("mm")
            S.mm([(ps[:, bk, :], w[:, (u * 8 + dc) * 128:(u * 8 + dc + 1) * 128], xh[:, dc, :], dc == 0, dc == 7) for dc in range(DC)],
                 reads=[wb] + xhb, writes=[pb[bk]])
            S.op("act", lambda e, bk=bk, u=u: e.activation(out=cqf[:, u, :], in_=ps[:, bk, :], func=AF.Copy), reads=[pb[bk]], writes=[cqfb[u]])
        ba, bb_ = self.bank("mm"), self.bank("mm")
        for off, bk in ((0, ba), (64, bb_)):
            S.mm([(ps[0:64, bk, :], w[:, (24 + dc) * 128 + off:(24 + dc) * 128 + off + 64], xh[:, dc, :], dc == 0, dc == 7)
                  for dc in range(DC)], reads=[wb] + xhb, writes=[pb[bk]])
        self.rope(T, ba, bb_, T["kr"][0:64, :], T["krb"])
        wb, w = self.wnext(("mla", j, "in", 1))
        for h in range(4):
            bk = self.bank("mm")
            S.mm([(ps[:, bk, :], w[:, (h * 8 + dc) * 128:(h * 8 + dc + 1) * 128], xh[:, dc, :], dc == 0, dc == 7) for dc in range(DC)],
                 reads=[wb] + xhb, writes=[pb[bk]])
            S.op("dve", lambda e, bk=bk, h=h: e.tensor_copy(out=big[:, 16 + h, :], in_=ps[:, bk, :]), reads=[pb[bk]], writes=[bigb[16 + h]])
        self.rmsnorm(T, [(cqfb[u], cqf[:, u, :], 128) for u in range(2)], G_QN + 2 * j, 256,
                     [(T["cqhb"][u], T["cqh"][:, u, :]) for u in range(2)])
        self.rmsnorm(T, [(cqfb[2], cqf[:, 2, :], 128)], G_KVN + j, 128, [(T["ckvb"], T["ckv"][:, :])])
        wb, w = self.wnext(("mla", j, "uq", 0))
        wkb, wk = self.wnext(("mla", j, "uk", 0))
        for h in range(8):
            bk = self.bank("mm")
            S.mm([(ps[:, bk, :], w[:, (h * 2 + kc) * 256:(h * 2 + kc) * 256 + 128], T["cqh"][:, kc, :], kc == 0, kc == 1) for kc in range(2)],
                 reads=[wb] + T["cqhb"], writes=[pb[bk]])
            qn, qnb = T["qn"][h % 2], T["qnb"][h % 2]
            S.op("act", lambda e, bk=bk, qn=qn: e.activation(out=qn[:, :], in_=ps[:, bk, :], func=AF.Copy), reads=[pb[bk]], writes=[qnb])
            ba, bb_ = self.bank("mm"), self.bank("mm")
            for off, bk2 in ((128, ba), (192, bb_)):
                S.mm([(ps[0:64, bk2, :], w[:, (h * 2 + kc) * 256 + off:(h * 2 + kc) * 256 + off + 64], T["cqh"][:, kc, :], kc == 0, kc == 1)
                      for kc in range(2)], reads=[wb] + T["cqhb"], writes=[pb[bk2]])
            self.rope(T, ba, bb_, big[0:64, 8 + h, :], bigb[8 + h])
            bk = self.bank("mm")
            S.mm([(ps[:, bk, :], wk[:, h * 128:(h + 1) * 128], qn[:, :], True, True)], reads=[wkb, qnb], writes=[pb[bk]])
            S.op("act", lambda e, bk=bk, h=h: e.activation(out=big[:, h, :], in_=ps[:, bk, :], func=AF.Copy), reads=[pb[bk]], writes=[bigb[h]])
        S.dma("pool", self.qscr[t].rearrange("p (a b) -> p a b", a=16), big[:, 0:16, :], T["sl_q"], reads=bigb[0:16], writes=[self.qscr_b[t]])
        S.dma("pool", self.exin[0:128, cols], T["ckv"][:, :], T["sl_k0"], reads=[T["ckvb"]], writes=[self.ex_b[t][0]])
        S.dma("pool", self.exin[128:192, cols], T["kr"][0:64, :], T["sl_k1"], reads=[T["krb"]], writes=[self.ex_b[t][1]])
        S.tr([(self.psb[:, tc * 128:(tc + 1) * 128], T["ckv"][:, tc * 128:(tc + 1) * 128], self.identb[:, :]) for tc in range(4)],
             reads=[T["ckvb"]], writes=[self.psbb])
        S.op("act", lambda e: e.activation(out=T["vst"][:, :], in_=self.psb[:, 0:512], func=AF.Copy), reads=[self.psbb], writes=[T["vstb"]])
        S.dma("pool", self.exin[192:320, cols], T["vst"][:, :], T["sl_v"], reads=[T["vstb"]], writes=[self.ex_b[t][2]])
        self.xa_attn(T, l)
        S.dma("pool", self.xascr[t].rearrange("p (a b) -> p a b", a=4), big[:, 16:20, :], T["sl_xa"], reads=bigb[16:20], writes=[self.xascr_b[t]])
        S.dma("pool", self.xscr[t].rearrange("p (a b) -> p a b", a=DC), T["xT"][:, :, :], T["sl_xst"], reads=T["xb"], writes=[self.xscr_b[t]])

    def mla_out(self, T, j, t):
        S, ps, pb = self.S, self.ps, self.pb
        big, bigb = T["big"], T["bigb"]
        S.dma("pool", T["xT"][:, :, :], self.xscr[t].rearrange("p (a b) -> p a b", a=DC), T["sl_xld"], reads=[self.xscr_b[t]], writes=T["xb"])
        S.dma("pool", big[:, 0:8, :], self.olscr[t].rearrange("p (a b) -> p a b", a=8), T["sl_old"], reads=[self.olscr_b[t]], writes=bigb[0:8])
        S.dma("pool", big[:, 16:20, :], self.xascr[t].rearrange("p (a b) -> p a b", a=4), T["sl_xald"], reads=[self.xascr_b[t]], writes=bigb[16:20])
        wb, w = self.wnext(("mla", j, "uv", 0))
        for h in range(8):
            bk = self.bank("mm")
            S.mm([(ps[:, bk, :], w[:, h * 128:(h + 1) * 128], big[:, h, :], True, True)], reads=[wb, bigb[h]], writes=[pb[bk]])
            S.op("act", lambda e, bk=bk, h=h: e.activation(out=big[:, 8 + h, :], in_=ps[:, bk, :], func=AF.Copy), reads=[pb[bk]], writes=[bigb[8 + h]])
        for k in range(4):
            wb, w = self.wnext(("mla", j, "out", k))
            for dd in range(2):
                dc = 2 * k + dd
                bo = self.bank("acc")
                S.mm([(ps[:, bo, :], w[:, (dd * 12 + kt) * 128:(dd * 12 + kt + 1) * 128], big[:, 8 + kt, :], kt == 0, kt == 11)
                      for kt in range(12)], reads=[wb] + bigb[8:20], writes=[pb[bo]])
                S.op("dve", lambda e, bo=bo, dc=dc: e.tensor_tensor(out=T["xT"][:, dc, :], in0=ps[:, bo, :], in1=T["xT"][:, dc, :], op=ALU.add),
                     reads=[pb[bo], T["xb"][dc]], writes=[T["xb"][dc]])

    def base_tensors(self, A):
        T = {}
        T["xT"] = A.sb("xT", [128, DC, TT], F32)
        T["xb"] = [Buf("x%d" % c) for c in range(DC)]
        T["xh"] = A.sb("xh", [128, DC, TT], BF16)
        T["xhb"] = [Buf("xh%d" % c) for c in range(DC)]
        T["big"] = A.sb("big", [128, FC, TT], BF16)
        T["bigb"] = [Buf("big%d" % c) for c in range(FC)]
        T["sq"] = A.sb("sq", [128, DC, TT], BF16)
        T["sqb"] = [Buf("sq%d" % c) for c in range(DC)]
        T["r"] = A.sb("r", [128, TT], F32)
        T["rb"] = Buf("r")
        T["rtmp"] = A.sb("rtmp", [128, TT], F32)
        T["rtmpb"] = Buf("rtmp")
        T["tmp"] = [A.sb("tmp%d" % i, [128, TT], F32) for i in range(2)]
        T["tmpb"] = [Buf("tmp%d" % i) for i in range(2)]
        T["tmpi"] = 0
        T["pt"] = [A.sb("pt%d" % i, [128, TT], BF16) for i in range(3)]
        T["ptb"] = [Buf("pt%d" % i) for i in range(3)]
        T["pti"] = 0
        return T

    def sgu_tensors(self, A, T, j):
        S = self.S
        T["vtf"] = A.sb("vtf", [128, 4, 1536], F32)
        T["vtfb"] = [[Buf("vtf%d_%d" % (tc, vb)) for vb in range(3)] for tc in range(4)]
        T["vsq"] = A.sb("vsq", [128, 1536], F32)
        T["vsqb"] = Buf("vsq")
        T["ssv"] = A.sb("ssv", [128, 4], F32)
        T["ssvb"] = Buf("ssv")
        T["vh"] = A.sb("vh", [128, 4, 1536], BF16)
        T["vhb"] = [Buf("vh%d" % tc) for tc in range(4)]
        T["gvb"] = A.sb("gvb", [128, 1536], F32)
        T["bsbt"] = A.sb("bsbt", [128, 1024], F32)
        T["sgc"] = Buf("sgc")
        sl = S.slot("sgc%d" % j)
        S.dma("sp", T["gvb"][:, :], self.vnb[j], sl, writes=[T["sgc"]])
        S.dma("sp", T["bsbt"][:, :], self.bsb[j], sl, writes=[T["sgc"]])

    def mlap_tensors(self, A, T, tag):
        S = self.S
        T["cqf"] = A.sb("cqf", [128, 3, TT], F32)
        T["cqfb"] = [Buf("cqf%d" % u) for u in range(3)]
        T["cqh"] = A.sb("cqh", [128, 2, TT], BF16)
        T["cqhb"] = [Buf("cqh%d" % u) for u in range(2)]
        T["ckv"] = A.sb("ckv", [128, TT], BF16)
        T["ckvb"] = Buf("ckv")
        T["kr"] = A.sb("kr", [128, TT], BF16)
        T["krb"] = Buf("kr")
        T["qn"] = [A.sb("qn%d" % i, [128, TT], BF16) for i in range(2)]
        T["qnb"] = [Buf("qn%d" % i) for i in range(2)]
        T["rC"] = A.sb("rC", [128, TT], F32)
        T["rS"] = A.sb("rS", [128, TT], F32)
        T["ropeb"] = Buf("rope")
        T["rt1"] = A.sb("rt1", [128, TT], F32)
        T["rt2"] = A.sb("rt2", [128, TT], F32)
        T["rt1b"], T["rt2b"] = Buf("rt1"), Buf("rt2")
        T["vst"] = A.sb("vst", [128, TT], BF16)
        T["vstb"] = Buf("vst")
        for nm in ("rope", "q", "k0", "k1", "v", "xa", "xst"):
            T["sl_" + nm] = S.slot(tag + nm)

    def phase0(self):
        S, ps, pb = self.S, self.ps, self.pb
        A = Alloc(self.nc, "p0_")
        T = self.base_tensors(A)
        self.start_stream(A, "P0", 1)
        mtok = A.sb("mtok", [128, 2, D], F32)
        mtb = Buf("mtok")
        sl = S.slot("mtok")
        S.dma("pool", mtok[:, :, :], self.mem[:, :].rearrange("(a p) d -> p a d", p=128), sl, writes=[mtb])
        for dc in range(DC):
            bk = self.bank("mm")
            S.tr([(ps[:, bk, a * 128:(a + 1) * 128], mtok[:, a, dc * 128:(dc + 1) * 128], self.identf[:, :]) for a in range(2)],
                 reads=[mtb], writes=[pb[bk]])
            S.op("act", lambda e, bk=bk, dc=dc: e.activation(out=T["xT"][:, dc, 0:NMEM], in_=ps[:, bk, 0:NMEM], func=AF.Copy),
                 reads=[pb[bk]], writes=[T["xb"][dc]])
        for l in range(4):
            self.rmsnorm(T, [(T["xb"][c], T["xT"][:, c, 0:NMEM], 128) for c in range(DC)], G_MEM + l * 8, 1024,
                         [(T["xhb"][c], T["xh"][:, c, 0:NMEM]) for c in range(DC)], N=NMEM)
            wb, w = self.wnext(("mem", l, "k", 0))
            for h in range(4):
                bk = self.bank("mm")
                S.mm([(ps[:, bk, 0:NMEM], w[:, (h * 8 + dc) * 128:(h * 8 + dc + 1) * 128], T["xh"][:, dc, 0:NMEM], dc == 0, dc == 7)
                      for dc in range(DC)], reads=[wb] + T["xhb"], writes=[pb[bk]])
                S.op("act", lambda e, bk=bk, h=h, l=l: e.activation(out=self.KmT[:, l, h, :], in_=ps[:, bk, 0:NMEM], func=AF.Copy),
                     reads=[pb[bk]], writes=[self.kmb[l]])
            wb, w = self.wnext(("mem", l, "v", 0))
            for mc in range(2):
                bk = self.bank("mm")
                S.mm([(ps[:, bk, :], T["xh"][:, dc, mc * 128:(mc + 1) * 128], w[:, dc * 512:(dc + 1) * 512], dc == 0, dc == 7)
                      for dc in range(DC)], reads=[wb] + T["xhb"], writes=[pb[bk]])
                S.op("act", lambda e, bk=bk, mc=mc, l=l: e.activation(out=self.Vm[:, l, mc, :], in_=ps[:, bk, :], func=AF.Copy),
                     reads=[pb[bk]], writes=[self.vmb[l]])
        S.barrier()
        A.close()

    def phaseA(self):
        S, ps, pb = self.S, self.ps, self.pb
        A = Alloc(self.nc, "pA_")
        T = self.base_tensors(A)
        self.start_stream(A, "A", self.NT)
        self.sgu_tensors(A, T, 0)
        self.mlap_tensors(A, T, "A")
        sl_x = S.slot("Ax")
        allv = [b for row in T["vtfb"] for b in row]
        xtok = T["vtf"]
        for t in range(self.NT):
            S.dma("pool", xtok[:, :, 0:D], self.x[t * TT:(t + 1) * TT, :].rearrange("(a p) d -> p a d", p=128), sl_x, writes=allv)
            for dc in range(DC):
                bk = self.bank("mm")
                S.tr([(ps[:, bk, a * 128:(a + 1) * 128], xtok[:, a, dc * 128:(dc + 1) * 128], self.identf[:, :]) for a in range(4)],
                     reads=allv, writes=[pb[bk]])
                S.op("act", lambda e, bk=bk, dc=dc: e.activation(out=T["xT"][:, dc, :], in_=ps[:, bk, :], func=AF.Copy),
                     reads=[pb[bk]], writes=[T["xb"][dc]])
            self.ffn(T, 0, 1)
            self.sgu(T, 0)
            self.ffn(T, 0, 2)
            self.ffn(T, 1, 1)
            self.mla_proj(T, 0, t)
        S.barrier()
        A.close()

    def phaseB2(self):
        S = self.S
        A = Alloc(self.nc, "pB_")
        T = self.base_tensors(A)
        self.start_stream(A, "B2", self.NT)
        self.sgu_tensors(A, T, 1)
        self.mlap_tensors(A, T, "B")
        for nm in ("xld", "old", "xald"):
            T["sl_" + nm] = S.slot("B" + nm)
        for t in range(self.NT):
            self.mla_out(T, 0, t)
            self.ffn(T, 1, 2)
            self.ffn(T, 2, 1)
            self.sgu(T, 1)
            self.ffn(T, 2, 2)
            self.ffn(T, 3, 1)
            self.mla_proj(T, 1, t)
        S.barrier()
        A.close()

    def phaseC2(self):
        S, ps, pb = self.S, self.ps, self.pb
        A = Alloc(self.nc, "pC_")
        T = self.base_tensors(A)
        self.start_stream(A, "C2", self.NT)
        for nm in ("xld", "old", "xald"):
            T["sl_" + nm] = S.slot("C" + nm)
        yT = A.sb("yT", [128, DC, TT], F32)
        yTb = [Buf("yT%d" % c) for c in range(DC)]
        yst = [A.sb("yst%d" % i, [128, D], F32) for i in range(2)]
        ystb = [Buf("yst%d" % i) for i in range(2)]
        sl_y = [S.slot("yst%d" % i) for i in range(2)]
        yb = Buf("y")
        k = 0
        for t in range(self.NT):
            self.mla_out(T, 1, t)
            self.ffn(T, 3, 2)
            self.rmsnorm(T, [(T["xb"][c], T["xT"][:, c, :], 128) for c in range(DC)], G_FIN, 1024,
                         [(yTb[c], yT[:, c, :]) for c in range(DC)])
            for tc in range(4):
                i = k % 2
                k += 1
                for half in range(2):
                    bk = self.bank("mm")
                    S.tr([(ps[:, bk, a * 128:(a + 1) * 128], yT[:, half * 4 + a, tc * 128:(tc + 1) * 128], self.identf[:, :]) for a in range(4)],
                         reads=yTb, writes=[pb[bk]])
                    S.op("act", lambda e, bk=bk, i=i, half=half: e.activation(out=yst[i][:, half * 512:(half + 1) * 512], in_=ps[:, bk, :], func=AF.Copy),
                         reads=[pb[bk]], writes=[ystb[i]])
                S.dma("pool", self.y[t * TT + tc * 128:t * TT + (tc + 1) * 128, :], yst[i][:, :], sl_y[i], reads=[ystb[i]], writes=[yb])
        S.barrier()
        A.close()

    def attention(self, tag):
        S, ps, pb = self.S, self.ps, self.pb
        TOK, NT, NKT = self.TOK, self.NT, self.NKT
        A = Alloc(self.nc, "at%s_" % tag)
        KT0 = A.sb("KT0", [128, self.NK], BF16)
        KT1 = A.sb("KT1", [128, self.NK], BF16)
        Vt = A.sb("Vt", [128, NKT, 128], BF16)
        km = A.sb("km", [128, NKT], F32)
        kb = Buf("K")
        sl = S.slot("atk" + tag)
        exr = [bb for row in self.ex_b for bb in row]
        S.dma("sp", KT0[:, :], self.exin[0:128, :], sl, reads=exr, writes=[kb])
        S.dma("sp", KT1[0:64, :], self.exin[128:192, :], sl, reads=exr, writes=[kb])
        S.dma("sp", Vt[:, :, :].rearrange("p a b -> p (a b)"), self.exin[192:320, :], sl, reads=exr, writes=[kb])
        S.dma("sp", km[:, :], self.kmask[:, :], sl, writes=[kb])
        q = [A.sb("q%d" % i, [128, 16, TT], BF16) for i in range(2)]
        qb = [Buf("q%d" % i) for i in range(2)]
        slq = [S.slot("atq%s%d" % (tag, i)) for i in range(2)]
        ost = [A.sb("ost%d" % i, [128, 8, TT], BF16) for i in range(2)]
        ostb = [[Buf("ost%d_%d" % (i, h)) for h in range(8)] for i in range(2)]
        slo = [S.slot("ato%s%d" % (tag, i)) for i in range(2)]
        NP = 6
        pt = [A.sb("pt%d" % i, [128, TT], BF16) for i in range(NP)]
        ptb = [Buf("apt%d" % i) for i in range(NP)]
        acc = {e: [A.sb("acc%s%d" % (e, i), [128, TT], F32) for i in range(2)] for e in ("dve", "pool")}
        accb = {e: [Buf("acc%s%d" % (e, i)) for i in range(2)] for e in ("dve", "pool")}
        rtmp = A.sb("rtmp", [128, TT], F32)
        rtmpb = Buf("artmp")
        scale = 192.0 ** -0.5

        def loadq(t):
            S.dma("pool", q[t % 2][:, :, :], self.qscr[t].rearrange("p (a b) -> p a b", a=16), slq[t % 2],
                  reads=[self.qscr_b[t]], writes=[qb[t % 2]])

        loadq(0)
        for t in range(NT):
            if t + 1 < NT:
                loadq(t + 1)
            qt, qtb = q[t % 2], qb[t % 2]
            for h in range(8):
                ob, db = 3 + h % 2, 5 + h % 2
                hp = h % 2

                def od(k):
                    S.mm([(ps[:, ob, :], Vt[:, k, :], pt[k % NP][:, :], k == 0, k == NKT - 1)],
                         reads=[ptb[k % NP], kb], writes=[pb[ob]])
                    e = "dve" if k % 2 == 0 else "pool"
                    at, ab = acc[e][hp], accb[e][hp]
                    if k < 2:
                        S.op(e, lambda en, at=at, k=k: en.tensor_copy(out=at[:, :], in_=pt[k % NP][:, :]),
                             reads=[ptb[k % NP]], writes=[ab])
                    else:
                        S.op(e, lambda en, at=at, k=k: en.tensor_tensor(out=at[:, :], in0=at[:, :], in1=pt[k % NP][:, :], op=ALU.add),
                             reads=[ptb[k % NP], ab], writes=[ab])

                for kt in range(NKT):
                    sb_ = kt % 3
                    S.mm([(ps[:, sb_, :], KT0[:, kt * 128:(kt + 1) * 128], qt[:, h, :], True, False),
                          (ps[:, sb_, :], KT1[0:64, kt * 128:(kt + 1) * 128], qt[0:64, 8 + h, :], False, True)],
                         reads=[kb, qtb], writes=[pb[sb_]])
                    S.op("act", lambda e, sb_=sb_, kt=kt: e.activation(out=pt[kt % NP][:, :], in_=ps[:, sb_, :], func=AF.Exp,
                                                                       bias=km[:, kt:kt + 1], scale=scale),
                         reads=[pb[sb_], kb], writes=[ptb[kt % NP]])
                    if kt >= 2:
                        od(kt - 2)
                od(NKT - 2)
                od(NKT - 1)
                S.mm([(ps[:, db, :], self.onesf[:, :], acc["dve"][hp][:, :], True, False),
                      (ps[:, db, :], self.onesf[:, :], acc["pool"][hp][:, :], False, True)],
                     reads=[accb["dve"][hp], accb["pool"][hp]], writes=[pb[db]])
                S.op("dve", lambda e, db=db: e.reciprocal(out=rtmp[:, :], in_=ps[:, db, :]), reads=[pb[db]], writes=[rtmpb])
                S.op("dve", lambda e, ob=ob, h=h, t=t: e.tensor_tensor(out=ost[t % 2][:, h, :], in0=ps[:, ob, :], in1=rtmp[:, :], op=ALU.mult),
                     reads=[pb[ob], rtmpb], writes=[ostb[t % 2][h]])
            S.dma("pool", self.olscr[t].rearrange("p (a b) -> p a b", a=8), ost[t % 2][:, :, :], slo[t % 2],
                  reads=ostb[t % 2], writes=[self.olscr_b[t]])
        S.barrier()
        A.close()

    def build(self):
        import os
        dbg = os.environ.get("KDEBUG", "")
        self.setup()
        if dbg == "setup":
            return self.nc
        self.convert()
        if dbg == "convert":
            return self.nc
        self.phase0()
        if dbg == "p0":
            return self.nc
        self.phaseA()
        if dbg == "A":
            return self.nc
        self.attention("1")
        self.phaseB2()
        self.attention("2")
        self.phaseC2()
        return self.nc


def _block_table():
    t = []
    for which in (1, 2):
        for l in range(4):
            t += [(("ffn", l, which, "in", k), 4096) for k in range(11)]
            t += [(("ffn", l, which, "out", dc), 2816) for dc in range(8)]
    for j in range(2):
        t += [(("sgu", j, "u", g2), 3072) for g2 in range(4)]
        t += [(("sgu", j, "v", vb), 4096) for vb in range(3)]
        t += [(("sgu", j, "xaq", 0), 4096), (("sgu", j, "ws", 0), 1024)]
        t += [(("sgu", j, "out", dc), 2560) for dc in range(8)]
        t += [(("mla", j, "in", 0), 4096), (("mla", j, "in", 1), 4096), (("mla", j, "uq", 0), 4096),
              (("mla", j, "uk", 0), 1024), (("mla", j, "uv", 0), 1024)]
        t += [(("mla", j, "out", k), 3072) for k in range(4)]
    for l in range(4):
        t += [(("mem", l, "k", 0), 4096), (("mem", l, "v", 0), 4096)]
    return t


NBLK = len(_block_table())


def _fm(wcols):
    k = wcols.shape[0] // 128
    return wcols.reshape(k, 128, wcols.shape[1]).transpose(1, 0, 2)


def _block_image(name, p, img):
    perm = np.concatenate([np.arange(32, 64), np.arange(0, 32)])
    kind = name[0]
    if kind == "ffn":
        _, l, which, io, k = name
        ffw = {1: (p["ffn1_w_in"], p["ffn1_w_out"]), 2: (p["ffn2_w_in"], p["ffn2_w_out"])}
        win, wout = ffw[which][0][l], ffw[which][1][l]
        if io == "in":
            for jj in range(2):
                j = 2 * k + jj
                V = img[:, jj * 2048:(jj + 1) * 2048].reshape(128, 8, 256)
                V[:, :, 0:128] = _fm(win[:, j * 128:(j + 1) * 128])
                V[:, :, 128:256] = _fm(win[:, DFF + j * 128:DFF + (j + 1) * 128])
        else:
            img[:, :2816].reshape(128, 22, 128)[:] = _fm(wout[:, k * 128:(k + 1) * 128])
    elif kind == "sgu":
        _, j, what, k = name
        win, wout = p["sgu_w_in"][j], p["sgu_w_out"][j]
        if what == "u":
            for gg in range(2):
                g = 2 * k + gg
                img[:, gg * 1536:(gg + 1) * 1536].reshape(128, 8, 192)[:] = _fm(win[:, 192 * g:192 * g + 192])
        elif what == "v":
            img.reshape(128, 8, 512)[:] = _fm(win[:, 1536 + 512 * k:1536 + 512 * k + 512])
        elif what == "xaq":
            for h in range(4):
                img[:, h * 1024:(h + 1) * 1024].reshape(128, 8, 128)[:] = _fm(win[:, 3072 + 128 * h:3072 + 128 * h + 128])
        elif what == "ws":
            img[:, :1024].reshape(128, 8, 128)[:] = np.transpose(p["sgu_w_s"][j], (2, 0, 1))
        else:
            Wc = wout[:, k * 128:(k + 1) * 128]
            gv = Wc[:1536].reshape(8, 192, 128)
            img[:, 0:1024].reshape(128, 8, 128)[:] = gv[:, :128].transpose(1, 0, 2)
            img[0:64, 1024:2048].reshape(64, 8, 128)[:] = gv[:, 128:192].transpose(1, 0, 2)
            img[:, 2048:2560].reshape(128, 4, 128)[:] = _fm(Wc[1536:])
    elif kind == "mla":
        _, j, what, k = name
        win = p["mla_w_in"][j]
        if what == "in" and k == 0:
            for u in range(3):
                img[:, u * 1024:(u + 1) * 1024].reshape(128, 8, 128)[:] = _fm(win[:, u * 128:(u + 1) * 128])
            V = img[:, 3072:4096].reshape(128, 8, 128)
            V[:, :, 0:64] = _fm(win[:, 384:448])
            V[:, :, 64:128] = _fm(win[:, 384 + perm])
        elif what == "in":
            for h in range(4):
                img[:, h * 1024:(h + 1) * 1024].reshape(128, 8, 128)[:] = _fm(win[:, 448 + 128 * h:448 + 128 * h + 128])
        elif what == "uq":
            wq = p["mla_w_uq"][j].reshape(2, 128, 8, 192)
            V = img.reshape(128, 8, 2, 256)
            V[..., 0:192] = wq.transpose(1, 2, 0, 3)
            V[..., 192:256] = wq[..., 128 + perm].transpose(1, 2, 0, 3)
        elif what == "uk":
            img[:, :1024].reshape(128, 8, 128)[:] = np.transpose(p["mla_w_uk"][j], (2, 1, 0))
        elif what == "uv":
            img[:, :1024].reshape(128, 8, 128)[:] = p["mla_w_uv"][j]
        else:
            for dd in range(2):
                dc = 2 * k + dd
                img[:, dd * 1536:(dd + 1) * 1536].reshape(128, 12, 128)[:] = _fm(p["mla_w_out"][j][:, dc * 128:(dc + 1) * 128])
    else:
        _, l, what, _k = name
        wm = p["w_mem_kv"][l]
        if what == "k":
            for h in range(4):
                img[:, h * 1024:(h + 1) * 1024].reshape(128, 8, 128)[:] = _fm(wm[:, h * 128:(h + 1) * 128])
        else:
            img.reshape(128, 8, 512)[:] = _fm(wm[:, 512:1024])


def _weight_images(p):
    out = np.zeros((NBLK, 128, WBLK), np.float32)
    for bid, (name, n) in enumerate(_block_table()):
        _block_image(name, p, out[bid])
    return out


def _rope_tables(pos0, n):
    inv = (1.0 / (np.float32(10000.0) ** (np.arange(0, 64, 2, dtype=np.float32) / np.float32(64)))).astype(np.float32)
    ang = (np.arange(pos0, pos0 + n, dtype=np.float32)[:, None] * inv[None, :]).astype(np.float32)
    c, s = np.cos(ang).astype(np.float32).T, np.sin(ang).astype(np.float32).T
    C = np.concatenate([c, c], 0)
    Sg = np.concatenate([-s, s], 0)
    return np.ascontiguousarray(C), np.ascontiguousarray(Sg)


def _prep_shared(p):
    f = lambda a: np.ascontiguousarray(np.asarray(a, dtype=np.float32))
    sh = {}
    gp = np.zeros((128, NG), np.float32)

    def put(col, vec):
        v = np.asarray(vec, np.float32).reshape(-1, 128)
        gp[:, col:col + v.shape[0]] = v.T

    for l in range(4):
        put(G_FFN1 + 8 * l, p["ffn1_norm"][l])
        put(G_MIX + 8 * l, p["mix_norm"][l])
        put(G_MEM + 8 * l, p["mem_norm"][l])
        put(G_FFN2 + 8 * l, p["ffn2_norm"][l])
    put(G_FIN, p["final_norm"])
    for j in range(2):
        put(G_QN + 2 * j, p["mla_q_norm"][j])
        put(G_KVN + j, p["mla_kv_norm"][j])
    sh["gpack"] = gp
    sh["vnb"] = f(np.broadcast_to(np.asarray(p["sgu_v_norm"], np.float32)[:, None, :], (2, 128, 1536)))
    sh["bsb"] = f(np.broadcast_to(np.asarray(p["sgu_b_s"], np.float32).reshape(2, 1, 1024), (2, 128, 1024)))
    sh["ident"] = np.eye(128, dtype=np.float32)
    return sh


_NC_CACHE = {}


def run(p, x_prompt, x_sample, mem_prompt, mem_sample):
    x_prompt, x_sample = np.asarray(x_prompt, np.float32), np.asarray(x_sample, np.float32)
    mem_prompt, mem_sample = np.asarray(mem_prompt, np.float32), np.asarray(mem_sample, np.float32)
    SP, TOK = x_prompt.shape[1], x_sample.shape[1]
    assert x_prompt.shape[0] == 4 and x_sample.shape[0] == 2 and SP <= TOK and TOK % TT == 0 and SP % 128 == 0
    NKT = TOK // 128
    p = {k: np.asarray(v, np.float32) for k, v in p.items()}
    sh = _prep_shared(p)
    sh["wsrc"] = _weight_images(p)
    ropeC, ropeS = _rope_tables(0, TOK)
    in_maps = []
    for c in range(8):
        m = dict(sh)
        xx = np.zeros((TOK, D), np.float32)
        km = np.zeros((128, NKT), np.float32)
        if c < 4:
            xx[:SP] = x_prompt[c]
            m["mem"] = np.ascontiguousarray(mem_prompt[c])
            km[:, SP // 128:] = NEG
        elif c < 6:
            xx[:] = x_sample[c - 4]
            m["mem"] = np.ascontiguousarray(mem_sample[c - 4])
        else:
            m["mem"] = np.ascontiguousarray(mem_sample[0])
        m["x"], m["kmask"], m["ropeC"], m["ropeS"] = xx, km, ropeC, ropeS
        in_maps.append(m)
    if TOK not in _NC_CACHE:
        _NC_CACHE[TOK] = Builder(TOK).build()
    nc = _NC_CACHE[TOK]
    res = run_bass_kernel_spmd(nc, in_maps, core_ids=list(range(8)))
    ys = [np.asarray(res.results[c]["y"], np.float32) for c in range(6)]
    y_prompt = np.stack([ys[c][:SP] for c in range(4)], 0)
    y_sample = np.stack(ys[4:6], 0)
    return y_prompt, y_sample


def kernel(x_prompt, x_sample, mem_prompt, mem_sample, **p):
    return run(p, x_prompt, x_sample, mem_prompt, mem_sample)
```
